# Optimizing a Trainium2 kernel written in Bass

```python
import math
import jax, jax.numpy as jnp
from jax import lax
import numpy as np

D_MODEL = 2048
BATCH = 4
SEQ = 4096
DEPTH = 2

N_EVEN = (DEPTH + 1) // 2
N_ODD = DEPTH // 2
ATTN_HEADS = 8
HEAD_DIM = 128
ATTN_WIDTH = ATTN_HEADS * HEAD_DIM
DILATED_BRANCHES = ((128, 1), (512, 4), (2048, 16))
HYENA_WIDTH = D_MODEL - ATTN_WIDTH
HYENA_POS_DIM = 33
HYENA_FILTER_ORDER = 64
HYENA_SHORT_WIDTH = 3
HYENA_DECAY_FAST = 0.3
HYENA_DECAY_SLOW = 1.5
HYENA_DECAY_TARGET = 1e-2
HYENA_FILTER_GAIN = 0.05
IN_WIDTH = 3 * ATTN_WIDTH + 3 * HYENA_WIDTH
POOL_WINDOWS = (2, 4, 8, 16)
POOL_GROUP = D_MODEL // len(POOL_WINDOWS)
N_EXPERTS = 16
EXPERT_FF = 2048
EC_CAPACITY_FACTOR = 2
DEEPNORM_ALPHA = (2 * DEPTH) ** 0.25
DEEPNORM_BETA = (8 * DEPTH) ** -0.25
LN_EPS = 1e-5
NEG_BIG = -1e30

kernel_name = "hybrid_dilated_hyena_pool_ecmoe_encoder"

F32 = jnp.float32


def layer_norm(x, g, b):
    xf = x.astype(F32)
    mu = jnp.mean(xf, axis=-1, keepdims=True)
    var = jnp.mean(jnp.square(xf - mu), axis=-1, keepdims=True)
    y = (xf - mu) * lax.rsqrt(var + LN_EPS) * g.astype(F32) + b.astype(F32)
    return y.astype(x.dtype)


def alibi_slopes(n_heads):
    return 2.0 ** (-(8.0 / n_heads) * jnp.arange(1, n_heads + 1, dtype=F32))


def dilated_branch(q, k, v, slopes, window, dilation):
    B, H, S, hd = q.shape
    n = window // (2 * dilation)
    Lc = S // dilation
    nb = -(-Lc // n)
    Lp = nb * n

    def to_classes(t):
        return t.reshape(B, H, Lc, dilation, hd).transpose(0, 1, 3, 2, 4)

    qc, kc, vc = to_classes(q), to_classes(k), to_classes(v)
    qb = jnp.pad(qc, ((0, 0), (0, 0), (0, 0), (0, Lp - Lc), (0, 0))).reshape(B, H, dilation, nb, n, hd)

    def neighbour_blocks(t):
        tb = jnp.pad(t, ((0, 0), (0, 0), (0, 0), (n, n + Lp - Lc), (0, 0))).reshape(B, H, dilation, nb + 2, n, hd)
        return jnp.concatenate([tb[:, :, :, :-2], tb[:, :, :, 1:-1], tb[:, :, :, 2:]], axis=4)

    kb, vb = neighbour_blocks(kc), neighbour_blocks(vc)
    s = jnp.einsum('bhrjqe,bhrjke->bhrjqk', qb, kb) * (1.0 / math.sqrt(hd))
    qi = jnp.arange(n)[:, None]
    ki = jnp.arange(3 * n)[None, :]
    rel = ki - n - qi
    key_pos = jnp.arange(nb)[:, None, None] * n - n + ki[None]
    valid = (jnp.abs(rel)[None] <= n) & (key_pos >= 0) & (key_pos < Lc)
    dist = (jnp.abs(rel) * dilation).astype(F32)
    bias = -slopes[:, None, None, None, None] * dist
    s = jnp.where(valid, s + bias, NEG_BIG)
    m = jnp.max(s, axis=-1, keepdims=True)
    p = jnp.exp(s - m)
    l = jnp.sum(p, axis=-1, keepdims=True)
    o = jnp.einsum('bhrjqk,bhrjke->bhrjqe', p, vb) / l
    lse = (m + jnp.log(l))[..., 0]

    o = o.reshape(B, H, dilation, Lp, hd)[:, :, :, :Lc]
    o = jnp.moveaxis(o, 2, 3).reshape(B, H, S, hd)
    lse = lse.reshape(B, H, dilation, Lp)[:, :, :, :Lc]
    lse = jnp.moveaxis(lse, 2, 3).reshape(B, H, S)
    return o, lse


def dilated_attention(q, k, v):
    slopes = alibi_slopes(q.shape[1])
    outs, lses = [], []
    for window, dilation in DILATED_BRANCHES:
        o, lse = dilated_branch(q, k, v, slopes, window, dilation)
        outs.append(o)
        lses.append(lse)
    w = jax.nn.softmax(jnp.stack(lses), axis=0)
    return jnp.einsum('nbhs,nbhse->bhse', w, jnp.stack(outs))


def short_conv(u, w, b):
    C = u.shape[-1]
    y = lax.conv_general_dilated(u, w[:, None, :].astype(u.dtype), window_strides=(1,),
                                 padding=((HYENA_SHORT_WIDTH // 2, HYENA_SHORT_WIDTH // 2),),
                                 dimension_numbers=('NWC', 'WIO', 'NWC'), feature_group_count=C)
    return y + b.astype(u.dtype)


def hyena_filters(L, w1, b1, w2, b2, w3, b3, w4, freq):
    t = jnp.linspace(0.0, 1.0, L, dtype=F32)[:, None]
    bands = (HYENA_POS_DIM - 1) // 2
    w_ang = 2.0 * math.pi * jnp.arange(L, dtype=F32)[:, None] / L
    f = jnp.linspace(1e-4, bands - 1, bands, dtype=F32)[None, :]
    z = jnp.concatenate([t, jnp.cos(f * w_ang), -jnp.sin(f * w_ang)], axis=-1)
    fr = freq.astype(F32)
    h = jnp.sin(fr * (z @ w1.astype(F32) + b1.astype(F32)))
    h = jnp.sin(fr * (h @ w2.astype(F32) + b2.astype(F32)))
    h = jnp.sin(fr * (h @ w3.astype(F32) + b3.astype(F32)))
    h = (h @ w4.astype(F32)).reshape(L, 2, HYENA_WIDTH)
    max_decay = math.log(HYENA_DECAY_TARGET) / HYENA_DECAY_FAST
    min_decay = math.log(HYENA_DECAY_TARGET) / HYENA_DECAY_SLOW
    deltas = jnp.linspace(min_decay, max_decay, HYENA_WIDTH, dtype=F32)
    decay = jnp.exp(-t * jnp.abs(deltas)[None, :])
    h = h * decay[:, None, :]
    return h[:, 0], h[:, 1]


def bidirectional_long_conv(u, h_fwd, h_bwd):
    L = u.shape[1]
    k_circ = jnp.concatenate([h_fwd, jnp.zeros_like(h_fwd[:1]), h_bwd[:0:-1]], axis=0)
    kf = jnp.fft.rfft(k_circ, axis=0)
    uf = jnp.fft.rfft(u, n=2 * L, axis=1)
    return jnp.fft.irfft(uf * kf[None], n=2 * L, axis=1)[:, :L]


def parallel_attention_hyena(x, w_in, w_out, conv_w, conv_b, fw1, fb1, fw2, fb2, fw3, fb3, fw4,
                             sin_freq, hy_bias):
    B, S, _ = x.shape
    proj = x @ w_in
    qkv = proj[..., :3 * ATTN_WIDTH].reshape(B, S, 3, ATTN_HEADS, HEAD_DIM).astype(F32)
    q = qkv[:, :, 0].transpose(0, 2, 1, 3)
    k = qkv[:, :, 1].transpose(0, 2, 1, 3)
    v = qkv[:, :, 2].transpose(0, 2, 1, 3)
    attn = dilated_attention(q, k, v).transpose(0, 2, 1, 3).reshape(B, S, ATTN_WIDTH)

    hy = short_conv(proj[..., 3 * ATTN_WIDTH:], conv_w, conv_b)
    x0 = hy[..., :HYENA_WIDTH]
    x1 = hy[..., HYENA_WIDTH:2 * HYENA_WIDTH]
    hv = hy[..., 2 * HYENA_WIDTH:]
    h_fwd, h_bwd = hyena_filters(S, fw1, fb1, fw2, fb2, fw3, fb3, fw4, sin_freq)
    z = (hv * x1).astype(F32)
    z = bidirectional_long_conv(z, h_fwd, h_bwd) + z * hy_bias.astype(F32)
    hyena = x0.astype(F32) * z

    mixed = jnp.concatenate([attn, hyena], axis=-1).astype(x.dtype)
    return mixed @ w_out


def multiscale_pool(x, pool_w, pool_scale):
    B, S, D = x.shape
    xf = x.astype(F32)
    cs = jnp.concatenate([jnp.zeros((B, 1, D), F32), lax.cumsum(xf, axis=1)], axis=1)
    pos = jnp.arange(S)
    outs = []
    for g, win in enumerate(POOL_WINDOWS):
        sl = slice(g * POOL_GROUP, (g + 1) * POOL_GROUP)
        lo = jnp.clip(pos - win // 2, 0, S)
        hi = jnp.clip(pos + win // 2, 0, S)
        csg = cs[..., sl]
        mean = (csg[:, hi] - csg[:, lo]) / (hi - lo).astype(F32)[None, :, None]
        outs.append(jnp.einsum('bsc,cd->bsd', mean - xf[..., sl], pool_w[g].astype(F32)))
    return (jnp.concatenate(outs, axis=-1) * pool_scale.astype(F32)).astype(x.dtype)


def expert_choice_ffn(x, router_w, w1, w3, w2):
    B, S, D = x.shape
    cap = EC_CAPACITY_FACTOR * S // N_EXPERTS
    aff = jax.nn.softmax((x @ router_w).astype(F32), axis=-1)
    gate, idx = lax.top_k(aff.transpose(0, 2, 1), cap)
    bidx = jnp.arange(B)[:, None, None]
    xs = x[bidx, idx]
    h = jax.nn.silu(jnp.einsum('becd,edf->becf', xs, w1)) * jnp.einsum('becd,edf->becf', xs, w3)
    y = jnp.einsum('becf,efd->becd', h, w2) * gate[..., None].astype(x.dtype)
    return jnp.zeros_like(x).at[bidx, idx].add(y)


def setup_inputs(seed: int = 0) -> dict:
    key = jax.random.key(seed)
    ks = jax.random.split(key, 24)

    def nrm(i, shape, scale):
        return jax.random.normal(ks[i], shape, F32) * scale

    ne, no = N_EVEN, N_ODD
    hw, fo, pd = HYENA_WIDTH, HYENA_FILTER_ORDER, HYENA_POS_DIM
    beta = DEEPNORM_BETA
    return {
        "x": nrm(0, (BATCH, SEQ, D_MODEL), 1.0),
        "mix_w_in": nrm(1, (ne, D_MODEL, IN_WIDTH), D_MODEL ** -0.5),
        "mix_w_out": nrm(2, (ne, D_MODEL, D_MODEL), beta * D_MODEL ** -0.5),
        "hy_conv_w": nrm(3, (ne, HYENA_SHORT_WIDTH, 3 * hw), HYENA_SHORT_WIDTH ** -0.5),
        "hy_conv_b": nrm(4, (ne, 3 * hw), 0.02),
        "hy_ffn_w1": nrm(5, (ne, pd, fo), pd ** -0.5),
        "hy_ffn_b1": nrm(6, (ne, fo), 0.02),
        "hy_ffn_w2": nrm(7, (ne, fo, fo), fo ** -0.5),
        "hy_ffn_b2": nrm(8, (ne, fo), 0.02),
        "hy_ffn_w3": nrm(9, (ne, fo, fo), fo ** -0.5),
        "hy_ffn_b3": nrm(10, (ne, fo), 0.02),
        "hy_ffn_w4": nrm(11, (ne, fo, 2 * hw), HYENA_FILTER_GAIN * fo ** -0.5),
        "hy_sin_freq": 1.0 + nrm(12, (ne, fo), 0.05),
        "hy_bias": nrm(13, (ne, hw), 0.1),
        "pool_w": nrm(14, (no, len(POOL_WINDOWS), POOL_GROUP, POOL_GROUP), beta * POOL_GROUP ** -0.5),
        "pool_scale": 1.0 + nrm(15, (no, D_MODEL), 0.1),
        "ln_mix_g": 1.0 + nrm(16, (DEPTH, D_MODEL), 0.02),
        "ln_mix_b": nrm(17, (DEPTH, D_MODEL), 0.02),
        "ln_ffn_g": 1.0 + nrm(18, (DEPTH, D_MODEL), 0.02),
        "ln_ffn_b": nrm(19, (DEPTH, D_MODEL), 0.02),
        "router_w": nrm(20, (DEPTH, D_MODEL, N_EXPERTS), D_MODEL ** -0.5),
        "exp_w1": nrm(21, (DEPTH, N_EXPERTS, D_MODEL, EXPERT_FF), D_MODEL ** -0.5),
        "exp_w3": nrm(22, (DEPTH, N_EXPERTS, D_MODEL, EXPERT_FF), D_MODEL ** -0.5),
        "exp_w2": nrm(23, (DEPTH, N_EXPERTS, EXPERT_FF, D_MODEL), beta * EXPERT_FF ** -0.5),
    }


def reference(x, mix_w_in, mix_w_out, hy_conv_w, hy_conv_b, hy_ffn_w1, hy_ffn_b1, hy_ffn_w2,
              hy_ffn_b2, hy_ffn_w3, hy_ffn_b3, hy_ffn_w4, hy_sin_freq, hy_bias, pool_w, pool_scale,
              ln_mix_g, ln_mix_b, ln_ffn_g, ln_ffn_b, router_w, exp_w1, exp_w3, exp_w2):
    for layer in range(DEPTH):
        i = layer // 2
        if layer % 2 == 0:
            mixed = parallel_attention_hyena(
                x, mix_w_in[i], mix_w_out[i], hy_conv_w[i], hy_conv_b[i],
                hy_ffn_w1[i], hy_ffn_b1[i], hy_ffn_w2[i], hy_ffn_b2[i], hy_ffn_w3[i], hy_ffn_b3[i],
                hy_ffn_w4[i], hy_sin_freq[i], hy_bias[i])
        else:
            mixed = multiscale_pool(x, pool_w[i], pool_scale[i])
        x = layer_norm(DEEPNORM_ALPHA * x + mixed, ln_mix_g[layer], ln_mix_b[layer])
        moe = expert_choice_ffn(x, router_w[layer], exp_w1[layer], exp_w3[layer], exp_w2[layer])
        x = layer_norm(DEEPNORM_ALPHA * x + moe, ln_ffn_g[layer], ln_ffn_b[layer])
    return x
```

```python
import math
import numpy as np
import ml_dtypes
from contextlib import ExitStack
import concourse.bass as bass
import concourse.mybir as mybir
from concourse.bass_utils import run_bass_kernel_spmd

F32 = mybir.dt.float32
BF16 = mybir.dt.bfloat16
ALU = mybir.AluOpType
AF = mybir.ActivationFunctionType
AX = mybir.AxisListType

S = 4096
D = 2048
NT = S // 128
E = 16
CAP = 2 * S // E
FF = 2048
HW = 1024
ALPHA = 4 ** 0.25
EPS = 1e-5
NCORE = 8
DIL = ((128, 1), (512, 4), (2048, 16))
NQ_DBG = 4


class Prog:
    ENGS = ("pe", "dve", "act", "pool", "sp")
    NDMA = 6

    def __init__(self, nc, stack):
        self.nc = nc
        self.rec = {e: [] for e in self.ENGS}
        self.sem = {e: stack.enter_context(nc.semaphore("s_" + e)) for e in self.ENGS}
        self.cnt = {e: 0 for e in self.ENGS}
        self.dsem = {e: [stack.enter_context(nc.semaphore("d_%s%d" % (e, i))) for i in range(self.NDMA)]
                     for e in ("sp", "act", "pool")}
        self.dval = {e: [0] * self.NDMA for e in self.dsem}
        self.drr = {e: 0 for e in self.dsem}
        self.known = {e: {} for e in self.ENGS}
        self.lastw = {}
        self.readers = {}
        self.final = []

    def _deps(self, reads, writes):
        deps = []
        for k in list(reads) + list(writes):
            deps.extend(self.lastw.get(k, ()))
        for k in writes:
            deps.extend(self.readers.get(k, ()))
        return deps

    def _commit(self, tok, reads, writes):
        for k in reads:
            self.readers.setdefault(k, []).append(tok)
        for k in writes:
            if self.readers.get(k):
                self.lastw[k] = [tok]
            else:
                lst = self.lastw.setdefault(k, [])
                lst.append(tok)
                if len(lst) > 64:
                    best = {}
                    for t in lst:
                        if t[0].name not in best or best[t[0].name][1] < t[1]:
                            best[t[0].name] = t
                    self.lastw[k] = list(best.values())
            self.readers[k] = []

    def _waits(self, eng, deps):
        need = {}
        for (sem, val, src) in deps:
            if src == eng and eng == "pe":
                continue
            key = sem.name
            if self.known[eng].get(key, 0) >= val:
                continue
            if key not in need or need[key][1] < val:
                need[key] = (sem, val)
        for key, (sem, val) in need.items():
            self.known[eng][key] = val
        return list(need.values())

    def op(self, eng, fn, reads=(), writes=()):
        waits = self._waits(eng, self._deps(reads, writes))
        self.cnt[eng] += 1
        sem = self.sem[eng]

        def emit(e, fn=fn, waits=waits, sem=sem):
            for (s, v) in waits:
                e.wait_ge(s, v)
            fn(e).then_inc(sem, 1)
        self.rec[eng].append(emit)
        self._commit((sem, self.cnt[eng], eng), reads, writes)

    def dma(self, eng, out, in_, reads=(), writes=(), is_output=False, **kw):
        deps = self._deps(reads, writes)
        i = self.drr[eng]
        self.drr[eng] = (i + 1) % self.NDMA
        dsem = self.dsem[eng][i]
        prev = self.dval[eng][i]
        if prev > 0:
            deps.append((dsem, prev, "dma"))
        waits = self._waits(eng, deps)
        self.dval[eng][i] = prev + 16

        def emit(e, waits=waits, dsem=dsem, out=out, in_=in_, kw=kw):
            for (s, v) in waits:
                e.wait_ge(s, v)
            e.dma_start(out=out, in_=in_, **kw).then_inc(dsem, 16)
        self.rec[eng].append(emit)
        tok = (dsem, prev + 16, "dma")
        self._commit(tok, reads, writes)
        if is_output:
            self.final.append(tok)

    def idma(self, fn, reads=(), writes=()):
        eng = "pool"
        deps = self._deps(reads, writes)
        i = self.drr[eng]
        self.drr[eng] = (i + 1) % self.NDMA
        dsem = self.dsem[eng][i]
        prev = self.dval[eng][i]
        if prev > 0:
            deps.append((dsem, prev, "dma"))
        waits = self._waits(eng, deps)
        self.dval[eng][i] = prev + 16

        def emit(e, waits=waits, dsem=dsem, fn=fn):
            for (s, v) in waits:
                e.wait_ge(s, v)
            fn(e).then_inc(dsem, 16)
        self.rec[eng].append(emit)
        self._commit((dsem, prev + 16, "dma"), reads, writes)

    def _all_tokens(self):
        toks = []
        for e in self.dsem:
            for i in range(self.NDMA):
                if self.dval[e][i] > 0:
                    toks.append((self.dsem[e][i], self.dval[e][i], "dma"))
        for e in self.ENGS:
            if self.cnt[e] > 0:
                toks.append((self.sem[e], self.cnt[e], "x"))
        return toks

    def barrier(self):
        toks = self._all_tokens()
        for eng in self.ENGS:
            waits = self._waits(eng, toks)

            def emit(e, waits=waits):
                for (s, v) in waits:
                    e.wait_ge(s, v)
            self.rec[eng].append(emit)
        self.lastw = {}
        self.readers = {}

    def finish(self):
        self.barrier()
        nc = self.nc
        with nc.Block() as block:
            @block.tensor
            def _(e):
                for f in self.rec["pe"]:
                    f(e)

            @block.vector
            def _(e):
                for f in self.rec["dve"]:
                    f(e)

            @block.scalar
            def _(e):
                for f in self.rec["act"]:
                    f(e)

            @block.gpsimd
            def _(e):
                for f in self.rec["pool"]:
                    f(e)

            @block.sync
            def _(e):
                for f in self.rec["sp"]:
                    f(e)


class K:
    def __init__(self, dbg=None):
        self.dbg = dbg
        nc = self.nc = bass.Bass("TRN2", target_bir_lowering=False)
        self.inp = {}
        self.st = ExitStack()
        self.P = Prog(nc, self.st)
        self.rrq = 0

    def din(self, name, shape, dt=F32):
        ap = self.nc.dram_tensor(name, list(shape), dt, kind="ExternalInput").ap()
        self.inp[name] = ap
        return ap

    def dscr(self, name, shape, dt=F32):
        if self.dbg and name in self.dbg:
            return self.nc.dram_tensor(name, list(shape), dt, kind="ExternalOutput").ap()
        return self.nc.dram_tensor(name, list(shape), dt).ap()

    def sb(self, stack, name, shape, dt=F32):
        self.uid = getattr(self, "uid", 0) + 1
        return stack.enter_context(self.nc.sbuf_tensor("sb%d_%s" % (self.uid, name), list(shape), dt))

    def q(self):
        self.rrq += 1
        return ("sp", "act")[self.rrq % 2]


def build(dbg=None, stages=None):
    k = K(dbg)
    nc, P = k.nc, k.P
    x_d = k.din("x", [S, D])
    xT_d = k.din("xT", [D, S])
    w_in_d = k.din("w_in", [D, 6144])
    w_out_d = k.din("w_out", [D, D])
    convw_d = k.din("convw", [3072, 3])
    convb_d = k.din("convb", [3072, 1])
    fw1_d = k.din("fw1", [33, 64]); fw2_d = k.din("fw2", [64, 64]); fw3_d = k.din("fw3", [64, 64])
    fw4_d = k.din("fw4", [64, 2048])
    fvec_d = k.din("fvec", [64, 4])
    hyb_d = k.din("hyb", [HW, 1])
    poolw_d = k.din("poolw", [4, 512, 512])
    pscale_d = k.din("pscale", [1, D])
    ln_d = k.din("ln", [8, D])
    rw_d = k.din("rw", [2, D, E])
    ew1_d = k.din("ew1", [2, E, D, FF]); ew3_d = k.din("ew3", [2, E, D, FF]); ew2_d = k.din("ew2", [2, E, FF, D])
    zf_d = k.din("zf", [33, S])
    tlin_d = k.din("tlin", [1, S])
    ndelta_d = k.din("ndelta", [HW, 1])
    abias_d = k.din("abias", [128, 8 * 3 * 256])
    iota_d = k.din("iota", [1, 512])
    pidx_d = k.din("pidx", [128, 4])
    ident_d = k.din("ident", [128, 128])
    pinv_d = k.din("pinv", [4, S])
    auxc_d = k.din("auxc", [128, NT, 2])
    ndrow_d = k.din("ndrow", [1, HW]); tcol_d = k.din("tcol", [128, NT]); m0col_d = k.din("m0col", [128, NT]); coef_d = k.din("coef", [128, 33])
    dftcF_d = k.din("dftcF", [33, 128, NT * 128], BF16); dftsF_d = k.din("dftsF", [33, 128, NT * 128], BF16)
    dftcI_d = k.din("dftcI", [16, 128, 33 * 256], BF16); dftsI_d = k.din("dftsI", [16, 128, 33 * 256], BF16)
    out_d = nc.dram_tensor("out", [S, D], F32, kind="ExternalOutput").ap()
    qk_s = k.dscr("qk_s", [3072, S], BF16)
    hy_s = k.dscr("hy_s", [3072, S], F32)
    mixT_s = k.dscr("mixT_s", [D, S], BF16)
    x1_s = k.dscr("x1_s", [S, D], F32)
    x1b_s = k.dscr("x1b_s", [S, D], BF16)
    slot_s = k.dscr("slot_s", [E, S], F32)
    affT_s = k.dscr("affT_s", [E, S], F32)
    y_s = k.dscr("y_s", [E * CAP, D], BF16)
    moe_s = k.dscr("moe_s", [S, D], F32)
    x2_s = k.dscr("x2_s", [S, D], F32)
    x2T_s = k.dscr("x2T_s", [D, S], F32)
    pl_s = k.dscr("pl_s", [S, D], F32)
    hs_s = k.dscr("hs_s", [S, HW], BF16); hd_s = k.dscr("hd_s", [S, HW], BF16); ztok_s = k.dscr("ztok_s", [S, HW], BF16)
    x0_s = k.dscr("x0_s", [HW, S], F32); z_s = k.dscr("z_s", [HW, S], F32)

    g = k.st
    ps = [g.enter_context(nc.psum_tensor("ps%d" % i, [128, 512], F32)) for i in range(7)]
    psb = g.enter_context(nc.psum_tensor("psb", [128, 1024], BF16))
    ident = k.sb(g, "ident", [128, 128], F32)
    identb = k.sb(g, "identb", [128, 128], BF16)
    onesb = k.sb(g, "onesb", [128, 128], BF16)
    P.dma("sp", ident[:], ident_d, writes=["ident"])
    P.dma("pool", identb[:], ident_d, writes=["identb"])
    P.op("dve", lambda e: e.memset(onesb[:], 1.0), writes=["onesb"])
    P.barrier()

    def PS(i):
        return ps[i], "ps%d" % i

    def on(s):
        return stages is None or s in stages

    def layer_norm(st, res, reskey, par, outkey, y):
        stats, mv, rstd, nmr = st["stats"][par], st["mv"][par], st["rstd"][par], st["nmr"][par]
        sk = "_%d" % par
        for c in range(4):
            P.op("dve", lambda e, c=c: e.bn_stats(stats[:, c * 6:(c + 1) * 6], res[:, c * 512:(c + 1) * 512]),
                 reads=[reskey], writes=["stats%d" % c + sk])
        P.op("dve", lambda e: e.bn_aggr(mv[:], stats[:]), reads=["stats%d" % c + sk for c in range(4)], writes=["mv" + sk])
        P.op("dve", lambda e: e.tensor_scalar(out=rstd[:], in0=mv[:, 1:2], scalar1=EPS, scalar2=None, op0=ALU.add), reads=["mv" + sk], writes=["rstd" + sk])
        P.op("act", lambda e: e.activation(out=rstd[:], in_=rstd[:], func=AF.Ln), reads=["rstd" + sk], writes=["rstd" + sk])
        P.op("act", lambda e: e.activation(out=rstd[:], in_=rstd[:], func=AF.Exp, scale=-0.5), reads=["rstd" + sk], writes=["rstd" + sk])
        P.op("dve", lambda e: e.scalar_tensor_tensor(out=nmr[:], in0=mv[:, 0:1], scalar=-1.0, in1=rstd[:], op0=ALU.mult, op1=ALU.mult),
             reads=["mv" + sk, "rstd" + sk], writes=["nmr" + sk])
        P.op("act", lambda e: e.activation(out=y[:], in_=res[:], func=AF.Identity, bias=nmr[:], scale=rstd[:]),
             reads=[reskey, "rstd" + sk, "nmr" + sk], writes=[outkey])
        P.op("dve", lambda e: e.tensor_tensor(out=y[:], in0=y[:], in1=st["g"][:], op=ALU.mult), reads=[outkey, "lng"], writes=[outkey])
        P.op("pool", lambda e: e.tensor_tensor(out=y[:], in0=y[:], in1=st["b"][:], op=ALU.add), reads=[outkey, "lnb"], writes=[outkey])

    def ln_tiles(stk, lrow):
        st = {}
        st["stats"] = [k.sb(stk, "ln_stats%d" % i, [128, 24]) for i in range(2)]; st["mv"] = [k.sb(stk, "ln_mv%d" % i, [128, 2]) for i in range(2)]
        st["rstd"] = [k.sb(stk, "ln_rstd%d" % i, [128, 1]) for i in range(2)]; st["nmr"] = [k.sb(stk, "ln_nmr%d" % i, [128, 1]) for i in range(2)]
        st["g"] = k.sb(stk, "ln_g", [128, D]); st["b"] = k.sb(stk, "ln_b", [128, D])
        P.dma("sp", st["g"][:], ln_d[lrow:lrow + 1, :].partition_broadcast(128), writes=["lng"])
        P.dma("sp", st["b"][:], ln_d[lrow + 1:lrow + 2, :].partition_broadcast(128), writes=["lnb"])
        return st

    def stage_proj():
        with ExitStack() as stk:
            xTb = k.sb(stk, "xTb", [128, 16, S], BF16)
            for kc in range(16):
                for hh in range(2):
                    P.dma("pool", xTb[:, kc, hh * 2048:(hh + 1) * 2048], xT_d[kc * 128:(kc + 1) * 128, hh * 2048:(hh + 1) * 2048],
                          writes=["xTb%d" % kc])
            wt = [k.sb(stk, "wt%d" % i, [128, 16, 128], BF16) for i in range(2)]
            ob = [k.sb(stk, "ob%d" % i, [128, S], F32) for i in range(2)]
            obb = [k.sb(stk, "obb%d" % i, [128, S], BF16) for i in range(2)]
            for cc in range(48):
                w = wt[cc % 2]; wk = "wt%d" % (cc % 2)
                P.dma("pool", w[:], w_in_d[:, cc * 128:(cc + 1) * 128].rearrange("(kc p) m -> p kc m", p=128), writes=[wk])
                o = (obb if cc < 24 else ob)[cc % 2]; okey = "ob%d_%d" % (cc % 2, cc < 24)
                for tb in range(8):
                    pt, pk = PS(tb % 4)
                    for kc in range(16):
                        P.op("pe", lambda e, pt=pt, w=w, kc=kc, tb=tb: e.matmul(pt[:], w[:, kc, :], xTb[:, kc, tb * 512:(tb + 1) * 512],
                                                                              start=(kc == 0), stop=(kc == 15)),
                             reads=[wk, "xTb%d" % kc], writes=[pk])
                    eng = "act" if tb % 2 == 0 else "dve"
                    if eng == "act":
                        P.op("act", lambda e, o=o, pt=pt, tb=tb: e.copy(out=o[:, tb * 512:(tb + 1) * 512], in_=pt[:]), reads=[pk], writes=[okey + "_%d" % tb])
                    else:
                        P.op("dve", lambda e, o=o, pt=pt, tb=tb: e.tensor_copy(out=o[:, tb * 512:(tb + 1) * 512], in_=pt[:]), reads=[pk], writes=[okey + "_%d" % tb])
                dst = qk_s[cc * 128:(cc + 1) * 128, :] if cc < 24 else hy_s[(cc - 24) * 128:(cc - 23) * 128, :]
                P.dma("sp", dst, o[:], reads=[okey + "_%d" % tb for tb in range(8)], writes=["scr_proj"])
            P.barrier()

    def stage_attn():
        with ExitStack() as stk:
            qT = k.sb(stk, "qT", [128, S], BF16); kT = k.sb(stk, "kT", [128, S], BF16); vT = k.sb(stk, "vT", [128, S], BF16)
            ab = k.sb(stk, "ab", [128, 3, 256], F32)
            acc_o = k.sb(stk, "acc_o", [128, S], F32); acc_l = k.sb(stk, "acc_l", [128, S], F32)
            vt = [k.sb(stk, "vt%d" % i, [128, 128], BF16) for i in range(3)]
            pt_ = [k.sb(stk, "pt%d" % i, [128, 256], BF16) for i in range(3)]
            tmp = [k.sb(stk, "atmp%d" % i, [128, 256], F32) for i in range(2)]
            ao = k.sb(stk, "ao", [128, S], BF16)
            scale = 1.0 / math.sqrt(128.0)
            cnt = 0
            blkc = [0]
            for h in range(8):
                P.dma("sp", qT[:], qk_s[h * 128:(h + 1) * 128, :], writes=["qT"])
                P.dma("act", kT[:], qk_s[1024 + h * 128:1024 + (h + 1) * 128, :], writes=["kT"])
                P.dma("sp", vT[:], qk_s[2048 + h * 128:2048 + (h + 1) * 128, :], writes=["vT"])
                P.dma("act", ab[:], abias_d[:, h * 768:(h + 1) * 768].rearrange("p (a b) -> p a b", a=3), writes=["ab"])
                P.op("dve", lambda e: e.memset(acc_o[:], 0.0), writes=["acc_o"])
                P.op("pool", lambda e: e.memset(acc_l[:], 0.0), writes=["acc_l"])
                for di, (_, d) in enumerate(DIL):
                    Lc = S // d
                    nt = Lc // 128
                    for r in range(d):
                        def cls(t, i0, i1, r=r, d=d):
                            return t[:, r + d * i0: r + d * (i1 - 1) + 1: d]

                        def block(m, cls=cls, nt=nt, Lc=Lc):
                            q0, q1 = max(0, 128 * m - 64), min(Lc, 128 * m + 64)
                            n = q1 - q0
                            bsel = blkc[0] % 2
                            blkc[0] += 1
                            po, pok = PS(3 + 2 * bsel); pl, plk = PS(4 + 2 * bsel)
                            terms = []
                            if m - 1 >= 0:
                                c0 = q0 - (128 * (m - 1) - 64)
                                terms.append(((m - 1) % 3, c0))
                            if m <= nt - 1:
                                c0 = q0 - (128 * m - 64)
                                terms.append((m % 3, c0))
                            for ti, (bi, c0) in enumerate(terms):
                                P.op("pe", lambda e, bi=bi, c0=c0, ti=ti, n=n: e.matmul(po[:, 0:n], vt[bi][:], pt_[bi][:, c0:c0 + n],
                                                                                   start=(ti == 0), stop=(ti == len(terms) - 1)),
                                     reads=["vt%d" % bi, "pt%d" % bi], writes=[pok])
                            for ti, (bi, c0) in enumerate(terms):
                                P.op("pe", lambda e, bi=bi, c0=c0, ti=ti, n=n: e.matmul(pl[:, 0:n], onesb[:], pt_[bi][:, c0:c0 + n],
                                                                                   start=(ti == 0), stop=(ti == len(terms) - 1)),
                                     reads=["pt%d" % bi], writes=[plk])
                            P.op("dve", lambda e, q0=q0, q1=q1, n=n: e.tensor_tensor(out=cls(acc_o, q0, q1), in0=cls(acc_o, q0, q1), in1=po[:, 0:n], op=ALU.add),
                                 reads=[pok, "acc_o"], writes=["acc_o"])
                            P.op("dve", lambda e, q0=q0, q1=q1, n=n: e.tensor_tensor(out=cls(acc_l, q0, q1), in0=cls(acc_l, q0, q1), in1=pl[:, 0:n], op=ALU.add),
                                 reads=[plk, "acc_l"], writes=["acc_l"])

                        for j in range(nt):
                            bi = j % 3
                            P.op("pe", lambda e, j=j, cls=cls: e.transpose(psb[:, 0:128], cls(vT, 128 * j, 128 * j + 128), identb[:]),
                                 reads=["vT", "identb"], writes=["psb"])
                            P.op("act", lambda e, bi=bi: e.copy(out=vt[bi][:], in_=psb[:, 0:128]), reads=["psb"], writes=["vt%d" % bi])
                            q0, q1 = max(0, 128 * j - 64), min(Lc, 128 * j + 192)
                            c0 = q0 - (128 * j - 64)
                            n = q1 - q0
                            pt, pk = PS(cnt % 3)
                            tm = tmp[cnt % 2]; tk = "atmp%d" % (cnt % 2)
                            cnt += 1
                            P.op("pe", lambda e, pt=pt, j=j, q0=q0, q1=q1, c0=c0, n=n, cls=cls: e.matmul(pt[:, c0:c0 + n], cls(kT, 128 * j, 128 * j + 128), cls(qT, q0, q1),
                                                                                             start=True, stop=True),
                                 reads=["kT", "qT"], writes=[pk])
                            P.op("dve", lambda e, pt=pt, tm=tm, c0=c0, n=n, di=di: e.scalar_tensor_tensor(out=tm[:, c0:c0 + n], in0=pt[:, c0:c0 + n], scalar=scale,
                                                                                                   in1=ab[:, di, c0:c0 + n], op0=ALU.mult, op1=ALU.add),
                                 reads=[pk, "ab"], writes=[tk])
                            P.op("act", lambda e, tm=tm, bi=bi, c0=c0, n=n: e.activation(out=pt_[bi][:, c0:c0 + n], in_=tm[:, c0:c0 + n], func=AF.Exp),
                                 reads=[tk], writes=["pt%d" % bi])
                            block(j)
                            if j == nt - 1:
                                block(nt)
                P.op("dve", lambda e: e.reciprocal(out=acc_l[:], in_=acc_l[:]), reads=["acc_l"], writes=["acc_l"])
                P.op("dve", lambda e: e.tensor_tensor(out=ao[:], in0=acc_o[:], in1=acc_l[:], op=ALU.mult), reads=["acc_o", "acc_l"], writes=["ao"])
                P.dma("sp", mixT_s[h * 128:(h + 1) * 128, :], ao[:], reads=["ao"], writes=["scr_mixA"])
            P.barrier()

    def stage_hyena():
        with ExitStack() as stk:
            PI = math.pi
            gT = k.sb(stk, "gT", [64, S], F32)
            w4 = k.sb(stk, "fw4", [64, 2048], F32)
            zstk = ExitStack()
            zf = k.sb(zstk, "zf", [33, S], F32)
            w1 = k.sb(zstk, "fw1", [33, 64], F32); w2 = k.sb(zstk, "fw2", [64, 64], F32); w3 = k.sb(zstk, "fw3", [64, 64], F32)
            fv = k.sb(zstk, "fv", [64, 4], F32); frb = k.sb(zstk, "frb", [64, 3], F32); npi = k.sb(zstk, "npi", [64, 1], F32)
            P.dma("sp", zf[:], zf_d, writes=["zf"])
            P.dma("act", w1[:], fw1_d, writes=["w1"]); P.dma("act", w2[:], fw2_d, writes=["w2"]); P.dma("act", w3[:], fw3_d, writes=["w3"])
            P.dma("sp", w4[:], fw4_d, writes=["w4"]); P.dma("sp", fv[:], fvec_d, writes=["fv"])
            P.op("dve", lambda e: e.memset(npi[:], -PI), writes=["npi"])
            for i in range(3):
                P.op("dve", lambda e, i=i: e.tensor_tensor(out=frb[:, i:i + 1], in0=fv[:, i:i + 1], in1=fv[:, 3:4], op=ALU.mult), reads=["fv"], writes=["frb%d" % i])
            u = [k.sb(zstk, "fu%d" % i, [64, 512], F32) for i in range(2)]
            ki = k.sb(zstk, "fki", [64, 512], mybir.dt.int32); kf = k.sb(zstk, "fkf", [64, 512], F32)
            for nb in range(8):
                src, srck, srcp = zf, "zf", 33
                for li, w in enumerate((w1, w2, w3)):
                    pt, pk = PS(li)
                    P.op("pe", lambda e, pt=pt, w=w, src=src, srcp=srcp, nb=nb, li=li: e.matmul(
                        pt[0:64, :], w[0:srcp, :], (src[0:srcp, nb * 512:(nb + 1) * 512] if li == 0 else src[0:64, :]), start=True, stop=True),
                         reads=["w%d" % (li + 1), srck], writes=[pk])
                    uu = u[li % 2]; uk = "fu%d" % (li % 2)
                    P.op("dve", lambda e, pt=pt, uu=uu, li=li: e.tensor_scalar(out=uu[:], in0=pt[0:64, :], scalar1=fv[:, 3:4], scalar2=frb[:, li:li + 1],
                                                                           op0=ALU.mult, op1=ALU.add), reads=[pk, "fv", "frb%d" % li], writes=[uk])
                    P.op("dve", lambda e, uu=uu: e.tensor_scalar(out=ki[:], in0=uu[:], scalar1=1.0 / (2.0 * PI), scalar2=8.0, op0=ALU.mult, op1=ALU.add),
                         reads=[uk], writes=["ki"])
                    P.op("dve", lambda e: e.tensor_copy(out=kf[:], in_=ki[:]), reads=["ki"], writes=["kf"])
                    P.op("dve", lambda e, uu=uu: e.scalar_tensor_tensor(out=uu[:], in0=kf[:], scalar=-2.0 * PI, in1=uu[:], op0=ALU.mult, op1=ALU.add),
                         reads=[uk, "kf"], writes=[uk])
                    P.op("dve", lambda e, uu=uu: e.tensor_scalar(out=uu[:], in0=uu[:], scalar1=16.0 * PI, scalar2=None, op0=ALU.add), reads=[uk], writes=[uk])
                    P.op("dve", lambda e, uu=uu: e.tensor_scalar(out=kf[:], in0=uu[:], scalar1=PI, scalar2=-2.0 * PI, op0=ALU.is_gt, op1=ALU.mult),
                         reads=[uk, "kf"], writes=["kf"])
                    P.op("dve", lambda e, uu=uu: e.tensor_tensor(out=uu[:], in0=uu[:], in1=kf[:], op=ALU.add), reads=[uk, "kf"], writes=[uk])
                    P.op("dve", lambda e, uu=uu: e.tensor_scalar(out=kf[:], in0=uu[:], scalar1=-PI, scalar2=2.0 * PI, op0=ALU.is_lt, op1=ALU.mult),
                         reads=[uk, "kf"], writes=["kf"])
                    P.op("dve", lambda e, uu=uu: e.tensor_tensor(out=uu[:], in0=uu[:], in1=kf[:], op=ALU.add), reads=[uk, "kf"], writes=[uk])
                    dst = gT[:, nb * 512:(nb + 1) * 512] if li == 2 else uu[:]
                    dk = "gT" if li == 2 else uk
                    P.op("act", lambda e, uu=uu, dst=dst: e.activation(out=dst, in_=uu[:], func=AF.Sin), reads=[uk], writes=[dk])
                    src, srck, srcp = uu, uk, 64
            P.barrier()
            zstk.close()
            with ExitStack() as fs:
                ndr = k.sb(fs, "ndr", [128, HW], F32)
                tcol = k.sb(fs, "tcol", [128, NT], F32); m0 = k.sb(fs, "m0", [128, NT], F32)
                P.dma("sp", ndr[:], ndrow_d.partition_broadcast(128), writes=["ndr"])
                P.dma("sp", tcol[:], tcol_d, writes=["tcol"]); P.dma("sp", m0[:], m0col_d, writes=["m0"])
                dec = [k.sb(fs, "fdec%d" % i, [128, HW], F32) for i in range(2)]
                hfb = [k.sb(fs, "fhf%d" % i, [128, HW], F32) for i in range(2)]
                hbb = [k.sb(fs, "fhb%d" % i, [128, HW], F32) for i in range(2)]
                hso = [k.sb(fs, "fhs%d" % i, [128, HW], BF16) for i in range(2)]
                hdo = [k.sb(fs, "fhd%d" % i, [128, HW], BF16) for i in range(2)]
                for lt in range(NT):
                    b = lt % 2
                    P.op("act", lambda e, b=b, lt=lt: e.activation(out=dec[b][:], in_=ndr[:], func=AF.Exp, scale=tcol[:, lt:lt + 1]),
                         reads=["ndr", "tcol"], writes=["fdec%d" % b])
                    for nbk in range(4):
                        pt, pk = PS(nbk)
                        P.op("pe", lambda e, pt=pt, nbk=nbk, lt=lt: e.matmul(pt[:], gT[:, lt * 128:(lt + 1) * 128], w4[:, nbk * 512:(nbk + 1) * 512], start=True, stop=True),
                             reads=["gT", "w4"], writes=[pk])
                        cs = slice((nbk % 2) * 512, (nbk % 2 + 1) * 512)
                        if nbk < 2:
                            P.op("dve", lambda e, pt=pt, b=b, cs=cs: e.tensor_tensor(out=hfb[b][:, cs], in0=pt[:], in1=dec[b][:, cs], op=ALU.mult),
                                 reads=[pk, "fdec%d" % b], writes=["fhf%d_%d" % (b, nbk % 2)])
                        else:
                            P.op("dve", lambda e, pt=pt, b=b, cs=cs, lt=lt: e.scalar_tensor_tensor(out=hbb[b][:, cs], in0=pt[:], scalar=m0[:, lt:lt + 1], in1=dec[b][:, cs],
                                                                                            op0=ALU.mult, op1=ALU.mult),
                                 reads=[pk, "fdec%d" % b, "m0"], writes=["fhb%d_%d" % (b, nbk % 2)])
                    P.op("dve", lambda e, b=b: e.tensor_tensor(out=hso[b][:], in0=hfb[b][:], in1=hbb[b][:], op=ALU.add),
                         reads=["fhf%d_0" % b, "fhf%d_1" % b, "fhb%d_0" % b, "fhb%d_1" % b], writes=["fhs%d" % b])
                    P.op("pool", lambda e, b=b: e.tensor_tensor(out=hdo[b][:], in0=hbb[b][:], in1=hfb[b][:], op=ALU.subtract),
                         reads=["fhf%d_0" % b, "fhf%d_1" % b, "fhb%d_0" % b, "fhb%d_1" % b], writes=["fhd%d" % b])
                    P.dma("sp", hs_s[lt * 128:(lt + 1) * 128, :], hso[b][:], reads=["fhs%d" % b], writes=["scr_hs"])
                    P.dma("act", hd_s[lt * 128:(lt + 1) * 128, :], hdo[b][:], reads=["fhd%d" % b], writes=["scr_hd"])
                P.barrier()
            with ExitStack() as cs_:
                uu3 = [k.sb(cs_, "hu%d" % i, [128, S], F32) for i in range(3)]
                cv = [k.sb(cs_, "hc%d" % i, [128, S], F32) for i in range(3)]
                cw = k.sb(cs_, "cw", [128, 3, 3], F32); cb = k.sb(cs_, "cb", [128, 3], F32)
                zt = [k.sb(cs_, "zt%d" % i, [128, NT, 128], BF16) for i in range(2)]
                for ct in range(8):
                    for i in range(3):
                        rows = slice(i * HW + ct * 128, i * HW + (ct + 1) * 128)
                        P.dma(k.q(), uu3[i][:], hy_s[rows, :], writes=["hu%d" % i])
                        P.dma(k.q(), cw[:, i, :], convw_d[rows, :], writes=["cw%d" % i])
                        P.dma(k.q(), cb[:, i:i + 1], convb_d[rows, :], writes=["cb%d" % i])
                    for i in range(3):
                        src, dst = uu3[i], cv[i]
                        eng = "dve"
                        P.op(eng, lambda e, src=src, dst=dst, i=i: e.tensor_scalar(out=dst[:], in0=src[:], scalar1=cw[:, i, 1:2], scalar2=cb[:, i:i + 1], op0=ALU.mult, op1=ALU.add),
                             reads=["hu%d" % i, "cw%d" % i, "cb%d" % i], writes=["hc%d" % i])
                        P.op(eng, lambda e, src=src, dst=dst, i=i: e.scalar_tensor_tensor(out=dst[:, 1:S], in0=src[:, 0:S - 1], scalar=cw[:, i, 0:1], in1=dst[:, 1:S], op0=ALU.mult, op1=ALU.add),
                             reads=["hu%d" % i, "cw%d" % i, "hc%d" % i], writes=["hc%d" % i])
                        P.op(eng, lambda e, src=src, dst=dst, i=i: e.scalar_tensor_tensor(out=dst[:, 0:S - 1], in0=src[:, 1:S], scalar=cw[:, i, 2:3], in1=dst[:, 0:S - 1], op0=ALU.mult, op1=ALU.add),
                             reads=["hu%d" % i, "cw%d" % i, "hc%d" % i], writes=["hc%d" % i])
                    z = uu3[0]
                    P.op("pool", lambda e: e.tensor_tensor(out=z[:], in0=cv[2][:], in1=cv[1][:], op=ALU.mult), reads=["hc1", "hc2", "hu0"], writes=["hu0"])
                    P.dma("sp", x0_s[ct * 128:(ct + 1) * 128, :], cv[0][:], reads=["hc0"], writes=["scr_x0"])
                    P.dma("act", z_s[ct * 128:(ct + 1) * 128, :], z[:], reads=["hu0"], writes=["scr_z"])
                    ztb = zt[ct % 2]; zk = "zt%d" % (ct % 2)
                    for tt in range(NT):
                        pt, pk = PS(tt % 4)
                        P.op("pe", lambda e, pt=pt, tt=tt: e.transpose(pt[:, 0:128], z[:, tt * 128:(tt + 1) * 128], ident[:]), reads=["hu0", "ident"], writes=[pk])
                        if tt % 2 == 0:
                            P.op("act", lambda e, pt=pt, tt=tt, ztb=ztb: e.copy(out=ztb[:, tt, :], in_=pt[:, 0:128]), reads=[pk], writes=[zk + "_%d" % tt])
                        else:
                            P.op("dve", lambda e, pt=pt, tt=tt, ztb=ztb: e.tensor_copy(out=ztb[:, tt, :], in_=pt[:, 0:128]), reads=[pk], writes=[zk + "_%d" % tt])
                    for q4 in range(4):
                        P.dma("sp", ztok_s[q4 * 1024:(q4 + 1) * 1024, ct * 128:(ct + 1) * 128].rearrange("(tt p) c -> p tt c", p=128), ztb[:, q4 * 8:(q4 + 1) * 8, :],
                              reads=[zk + "_%d" % tt for tt in range(q4 * 8, q4 * 8 + 8)], writes=["scr_ztok"])
                P.barrier()
            NF = 33
            with ExitStack() as ds:
                coef = k.sb(ds, "coef", [128, NF], F32)
                P.dma("sp", coef[:], coef_d, writes=["coef"])
                Aa = k.sb(ds, "dA", [128, NF, 256], BF16); Bb = k.sb(ds, "dB", [128, NF, 256], BF16)
                for cq in range(NQ_DBG):
                    ccols = slice(cq * 256, (cq + 1) * 256)
                    with ExitStack() as f1:
                        zc = k.sb(f1, "zc", [128, NT, 512], BF16); zs = k.sb(f1, "zs", [128, NT, 512], BF16)
                        for q4 in range(4):
                            rows = slice(q4 * 1024, (q4 + 1) * 1024)
                            tts = slice(q4 * 8, (q4 + 1) * 8)
                            P.dma("sp", zc[:, tts, 0:256], ztok_s[rows, ccols].rearrange("(tt p) c -> p tt c", p=128), writes=["zc"])
                            P.dma("act", zs[:, tts, 0:256], ztok_s[rows, ccols].rearrange("(tt p) c -> p tt c", p=128), writes=["zs"])
                            P.dma("sp", zc[:, tts, 256:512], hs_s[rows, ccols].rearrange("(tt p) c -> p tt c", p=128), writes=["zc"])
                            P.dma("act", zs[:, tts, 256:512], hd_s[rows, ccols].rearrange("(tt p) c -> p tt c", p=128), writes=["zs"])
                        tc_ = [k.sb(f1, "tc%d" % i, [128, NT, 128], BF16) for i in range(2)]
                        ts_ = [k.sb(f1, "ts%d" % i, [128, NT, 128], BF16) for i in range(2)]
                        kr = k.sb(f1, "kr", [128, 256], F32); ki_ = k.sb(f1, "kiq", [128, 256], F32)
                        t1 = k.sb(f1, "dt1", [128, 256], F32); t2 = k.sb(f1, "dt2", [128, 256], F32)
                        for ft in range(NF):
                            b = ft % 2
                            P.dma("sp", tc_[b][:], dftcF_d[ft].rearrange("p (tt f) -> p tt f", f=128), writes=["tc%d" % b])
                            P.dma("act", ts_[b][:], dftsF_d[ft].rearrange("p (tt f) -> p tt f", f=128), writes=["ts%d" % b])
                            pc, pck = PS(b); psn, psk = PS(2 + b)
                            for tt in range(NT):
                                P.op("pe", lambda e, pc=pc, b=b, tt=tt: e.matmul(pc[:], tc_[b][:, tt, :], zc[:, tt, :], start=(tt == 0), stop=(tt == NT - 1)),
                                     reads=["tc%d" % b, "zc"], writes=[pck])
                            for tt in range(NT):
                                P.op("pe", lambda e, psn=psn, b=b, tt=tt: e.matmul(psn[:], ts_[b][:, tt, :], zs[:, tt, :], start=(tt == 0), stop=(tt == NT - 1)),
                                     reads=["ts%d" % b, "zs"], writes=[psk])
                            P.op("act", lambda e, pc=pc, ft=ft: e.activation(out=kr[:], in_=pc[:, 256:512], func=AF.Identity, scale=coef[:, ft:ft + 1]), reads=[pck, "coef"], writes=["kr"])
                            P.op("act", lambda e, psn=psn, ft=ft: e.activation(out=ki_[:], in_=psn[:, 256:512], func=AF.Identity, scale=coef[:, ft:ft + 1]), reads=[psk, "coef"], writes=["kiq"])
                            P.op("dve", lambda e, pc=pc: e.tensor_tensor(out=t1[:], in0=pc[:, 0:256], in1=kr[:], op=ALU.mult), reads=[pck, "kr"], writes=["dt1"])
                            P.op("dve", lambda e, psn=psn: e.tensor_tensor(out=t2[:], in0=psn[:, 0:256], in1=ki_[:], op=ALU.mult), reads=[psk, "kiq"], writes=["dt2"])
                            P.op("dve", lambda e, ft=ft: e.tensor_tensor(out=Aa[:, ft, :], in0=t1[:], in1=t2[:], op=ALU.add), reads=["dt1", "dt2"], writes=["dA%d" % ft])
                            P.op("dve", lambda e, psn=psn: e.tensor_tensor(out=t1[:], in0=psn[:, 0:256], in1=kr[:], op=ALU.mult), reads=[psk, "kr", "dt1"], writes=["dt1"])
                            P.op("dve", lambda e, pc=pc: e.tensor_tensor(out=t2[:], in0=pc[:, 0:256], in1=ki_[:], op=ALU.mult), reads=[pck, "kiq", "dt2"], writes=["dt2"])
                            P.op("dve", lambda e, ft=ft: e.tensor_tensor(out=Bb[:, ft, :], in0=t1[:], in1=t2[:], op=ALU.subtract), reads=["dt1", "dt2"], writes=["dB%d" % ft])
                        P.barrier()
                    with ExitStack() as f2:
                        ic = [k.sb(f2, "ic%d" % i, [128, NF, 256], BF16) for i in range(2)]
                        isn = [k.sb(f2, "is%d" % i, [128, NF, 256], BF16) for i in range(2)]
                        x0t = [k.sb(f2, "x0t%d" % i, [128, 256], F32) for i in range(2)]
                        zt2 = [k.sb(f2, "zt2%d" % i, [128, 256], F32) for i in range(2)]
                        ot = [k.sb(f2, "hot%d" % i, [128, 256], BF16) for i in range(2)]
                        hbias = k.sb(f2, "hbias", [128, 2], F32)
                        for c2 in range(2):
                            P.dma("sp", hbias[:, c2:c2 + 1], hyb_d[cq * 256 + c2 * 128: cq * 256 + (c2 + 1) * 128, :], writes=["hbias"])
                        n = 0
                        for tb in range(16):
                            b = tb % 2
                            tcols = slice(tb * 256, (tb + 1) * 256)
                            P.dma("sp", ic[b][:], dftcI_d[tb].rearrange("p (ft t) -> p ft t", t=256), writes=["ic%d" % b])
                            P.dma("act", isn[b][:], dftsI_d[tb].rearrange("p (ft t) -> p ft t", t=256), writes=["is%d" % b])
                            for c2 in range(2):
                                pt, pk = PS(n % 4)
                                nb2 = n % 2; n += 1
                                crow = slice(cq * 256 + c2 * 128, cq * 256 + (c2 + 1) * 128)
                                P.dma("sp", x0t[nb2][:], x0_s[crow, tcols], writes=["x0t%d" % nb2])
                                P.dma("act", zt2[nb2][:], z_s[crow, tcols], writes=["zt2%d" % nb2])
                                for ft in range(NF):
                                    P.op("pe", lambda e, pt=pt, ft=ft, c2=c2, b=b: e.matmul(pt[:, 0:256], Aa[:, ft, c2 * 128:(c2 + 1) * 128], ic[b][:, ft, :], start=(ft == 0), stop=False),
                                         reads=["dA%d" % ft, "ic%d" % b], writes=[pk])
                                for ft in range(NF):
                                    P.op("pe", lambda e, pt=pt, ft=ft, c2=c2, b=b: e.matmul(pt[:, 0:256], Bb[:, ft, c2 * 128:(c2 + 1) * 128], isn[b][:, ft, :], start=False, stop=(ft == NF - 1)),
                                         reads=["dB%d" % ft, "is%d" % b], writes=[pk])
                                P.op("dve", lambda e, pt=pt, nb2=nb2, c2=c2: e.scalar_tensor_tensor(out=zt2[nb2][:], in0=zt2[nb2][:], scalar=hbias[:, c2:c2 + 1], in1=pt[:, 0:256],
                                                                                                 op0=ALU.mult, op1=ALU.add), reads=[pk, "zt2%d" % nb2, "hbias"], writes=["zt2%d" % nb2])
                                P.op("dve", lambda e, nb2=nb2: e.tensor_tensor(out=ot[nb2][:], in0=zt2[nb2][:], in1=x0t[nb2][:], op=ALU.mult),
                                     reads=["zt2%d" % nb2, "x0t%d" % nb2], writes=["hot%d" % nb2])
                                P.dma("sp", mixT_s[HW + cq * 256 + c2 * 128: HW + cq * 256 + (c2 + 1) * 128, tcols], ot[nb2][:], reads=["hot%d" % nb2], writes=["scr_mixH"])
                        P.barrier()
            P.barrier()

    def router_tail(stk, rt, y, ykey, tt):
        yT = rt["yT"]
        for kc in range(16):
            pt, pk = PS(4 + kc % 2)
            P.op("pe", lambda e, pt=pt, kc=kc: e.transpose(pt[:, 0:128], y[:, kc * 128:(kc + 1) * 128], ident[:]), reads=[ykey, "ident"], writes=[pk])
            eng = "act" if kc % 2 == 0 else "dve"
            if eng == "act":
                P.op("act", lambda e, pt=pt, kc=kc: e.copy(out=yT[:, kc, :], in_=pt[:, 0:128]), reads=[pk], writes=["yT%d" % kc])
            else:
                P.op("dve", lambda e, pt=pt, kc=kc: e.tensor_copy(out=yT[:, kc, :], in_=pt[:, 0:128]), reads=[pk], writes=["yT%d" % kc])
        pt, pk = PS(6)
        for kc in range(16):
            P.op("pe", lambda e, kc=kc: e.matmul(pt[:, 0:E], yT[:, kc, :], rt["rw"][:, kc, :], start=(kc == 0), stop=(kc == 15)),
                 reads=["yT%d" % kc, "rw"], writes=[pk])
        mx, sm, ex = rt["mx"], rt["sm"], rt["ex"]
        P.op("dve", lambda e: e.reduce_max(out=mx[:], in_=pt[:, 0:E], axis=AX.X), reads=[pk], writes=["mx"])
        P.op("dve", lambda e: e.tensor_scalar(out=ex[:], in0=pt[:, 0:E], scalar1=mx[:], scalar2=None, op0=ALU.subtract), reads=[pk, "mx"], writes=["ex"])
        P.op("act", lambda e: e.activation(out=ex[:], in_=ex[:], func=AF.Exp), reads=["ex"], writes=["ex"])
        P.op("dve", lambda e: e.reduce_sum(out=sm[:], in_=ex[:], axis=AX.X), reads=["ex"], writes=["sm"])
        P.op("dve", lambda e: e.reciprocal(out=sm[:], in_=sm[:]), reads=["sm"], writes=["sm"])
        P.op("dve", lambda e: e.tensor_scalar(out=rt["aff_tok"][:, tt, :], in0=ex[:], scalar1=sm[:], scalar2=None, op0=ALU.mult), reads=["ex", "sm"], writes=["aff_tok"])
        pt2, pk2 = PS(5)
        P.op("pe", lambda e: e.transpose(pt2[0:E, 0:128], rt["aff_tok"][:, tt, :], ident[:]), reads=["aff_tok", "ident"], writes=[pk2])
        P.op("dve", lambda e: e.tensor_copy(out=rt["affT"][:, tt * 128:(tt + 1) * 128], in_=pt2[0:E, 0:128]), reads=[pk2], writes=["affT"])

    def router_tiles(stk, layer, gl):
        rt = {}
        rt["yT"] = k.sb(stk, "yT", [128, 16, 128], F32)
        rt["rw"] = k.sb(stk, "rw", [128, 16, E], F32)
        P.dma("sp", rt["rw"][:], rw_d[layer].rearrange("(kc p) e -> p kc e", p=128), writes=["rw"])
        rt["mx"] = k.sb(stk, "mx", [128, 1]); rt["sm"] = k.sb(stk, "sm", [128, 1]); rt["ex"] = k.sb(stk, "ex", [128, E])
        rt["aff_tok"] = gl["aff_tok"]; rt["affT"] = k.sb(stk, "affT", [E, S], F32)
        return rt

    def stage_mix_out(layer, gl, mixer):
        with ExitStack() as stk:
            lt = ln_tiles(stk, layer * 4)
            rt = router_tiles(stk, layer, gl)
            xr = [k.sb(stk, "xr%d" % i, [128, D], F32) for i in range(2)]
            res2 = [k.sb(stk, "res%d" % i, [128, D], F32) for i in range(2)]
            y = [k.sb(stk, "yln%d" % i, [128, D], F32) for i in range(2)]
            yb = [k.sb(stk, "ylnb%d" % i, [128, D], BF16) for i in range(2)]
            mx_fn = mixer(stk)
            xin = x_d if layer == 0 else x2_s
            for tt in range(NT):
                b = tt % 2
                P.dma("act", xr[b][:], xin[tt * 128:(tt + 1) * 128, :], writes=["xr%d" % b])
                res = res2[b]
                mx_fn(tt, xr[b], "xr%d" % b, res, "res%d" % b)
                layer_norm(lt, res, "res%d" % b, b, "yln%d" % b, y[b])
                P.dma("sp", x1_s[tt * 128:(tt + 1) * 128, :], y[b][:], reads=["yln%d" % b], writes=["scr_x1"])
                P.op("pool", lambda e, b=b: e.tensor_copy(out=yb[b][:], in_=y[b][:]), reads=["yln%d" % b], writes=["ylnb%d" % b])
                P.dma("act", x1b_s[tt * 128:(tt + 1) * 128, :], yb[b][:], reads=["ylnb%d" % b], writes=["scr_x1b"])
                router_tail(stk, rt, y[b], "yln%d" % b, tt)
            P.dma("act", affT_s, rt["affT"][:], reads=["affT"], writes=["scr_affT"])
            P.barrier()

    def mixer_attnhy(stk):
        wo = k.sb(stk, "wo", [128, 16, D], BF16)
        for kc in range(16):
            P.dma("pool", wo[:, kc, :], w_out_d[kc * 128:(kc + 1) * 128, :], writes=["wo"])
        mt = [k.sb(stk, "mt%d" % i, [128, 16, 128], BF16) for i in range(2)]

        def fn(tt, xr, xrk, res, resk):
            m = mt[tt % 2]; mk = "mt%d" % (tt % 2)
            P.dma("sp", m[:], mixT_s[:, tt * 128:(tt + 1) * 128].rearrange("(kc p) t -> p kc t", p=128), writes=[mk])
            for nb in range(4):
                pt, pk = PS(nb)
                for kc in range(16):
                    P.op("pe", lambda e, pt=pt, kc=kc, nb=nb, m=m: e.matmul(pt[:], m[:, kc, :], wo[:, kc, nb * 512:(nb + 1) * 512], start=(kc == 0), stop=(kc == 15)),
                         reads=[mk, "wo"], writes=[pk])
                P.op("dve", lambda e, pt=pt, nb=nb, xr=xr: e.scalar_tensor_tensor(out=res[:, nb * 512:(nb + 1) * 512], in0=xr[:, nb * 512:(nb + 1) * 512], scalar=ALPHA,
                                                                                in1=pt[:], op0=ALU.mult, op1=ALU.add), reads=[pk, xrk], writes=[resk])
        return fn

    def mixer_pool(stk):
        pl = [k.sb(stk, "plr%d" % i, [128, D], F32) for i in range(2)]

        def fn(tt, xr, xrk, res, resk):
            p = pl[tt % 2]; pk = "plr%d" % (tt % 2)
            P.dma("sp", p[:], pl_s[tt * 128:(tt + 1) * 128, :], writes=[pk])
            P.op("dve", lambda e: e.scalar_tensor_tensor(out=res[:], in0=xr[:], scalar=ALPHA, in1=p[:], op0=ALU.mult, op1=ALU.add), reads=[pk, xrk], writes=[resk])
        return fn

    def stage_select(gl):
        with ExitStack() as stk:
            affT = k.sb(stk, "affT", [E, S], F32)
            P.dma("sp", affT[:], affT_s, writes=["affT"])
            lo = k.sb(stk, "lo", [E, 1]); hi = k.sb(stk, "hi", [E, 1]); mid = k.sb(stk, "mid", [E, 1]); cnt = k.sb(stk, "cnt", [E, 1])
            ge = k.sb(stk, "ge", [E, 1]); t1 = k.sb(stk, "t1", [E, 1])
            m = k.sb(stk, "msk", [E, S]); c2 = k.sb(stk, "csum", [E, S]); ones = k.sb(stk, "ones_s", [E, S])
            P.op("dve", lambda e: e.memset(lo[:], 0.0), writes=["lo"])
            P.op("dve", lambda e: e.memset(hi[:], 1.0), writes=["hi"])
            P.op("dve", lambda e: e.memset(ones[:], 1.0), writes=["ones_s"])
            for it in range(34):
                P.op("dve", lambda e: e.tensor_tensor(out=mid[:], in0=lo[:], in1=hi[:], op=ALU.add), reads=["lo", "hi"], writes=["mid"])
                P.op("dve", lambda e: e.tensor_scalar(out=mid[:], in0=mid[:], scalar1=0.5, scalar2=None, op0=ALU.mult), reads=["mid"], writes=["mid"])
                P.op("dve", lambda e: e.tensor_scalar(out=m[:], in0=affT[:], scalar1=mid[:], scalar2=None, op0=ALU.is_ge), reads=["affT", "mid"], writes=["msk"])
                P.op("dve", lambda e: e.reduce_sum(out=cnt[:], in_=m[:], axis=AX.X), reads=["msk"], writes=["cnt"])
                P.op("dve", lambda e: e.tensor_scalar(out=ge[:], in0=cnt[:], scalar1=float(CAP) - 0.5, scalar2=None, op0=ALU.is_ge), reads=["cnt"], writes=["ge"])
                P.op("dve", lambda e: e.tensor_tensor(out=t1[:], in0=mid[:], in1=lo[:], op=ALU.subtract), reads=["mid", "lo"], writes=["t1"])
                P.op("dve", lambda e: e.scalar_tensor_tensor(out=lo[:], in0=t1[:], scalar=ge[:], in1=lo[:], op0=ALU.mult, op1=ALU.add), reads=["t1", "ge", "lo"], writes=["lo"])
                P.op("dve", lambda e: e.tensor_tensor(out=t1[:], in0=hi[:], in1=mid[:], op=ALU.subtract), reads=["mid", "hi"], writes=["t1"])
                P.op("dve", lambda e: e.scalar_tensor_tensor(out=hi[:], in0=t1[:], scalar=ge[:], in1=mid[:], op0=ALU.mult, op1=ALU.add), reads=["t1", "ge", "mid"], writes=["hi"])
            P.op("dve", lambda e: e.tensor_scalar(out=m[:], in0=affT[:], scalar1=lo[:], scalar2=None, op0=ALU.is_ge), reads=["affT", "lo"], writes=["msk"])
            P.op("dve", lambda e: e.tensor_tensor_scan(out=c2[:], data0=ones[:], data1=m[:], initial=0.0, op0=ALU.mult, op1=ALU.add), reads=["msk", "ones_s"], writes=["csum"])
            P.op("dve", lambda e: e.tensor_tensor(out=c2[:], in0=c2[:], in1=m[:], op=ALU.mult), reads=["msk", "csum"], writes=["csum"])
            P.op("dve", lambda e: e.tensor_scalar(out=c2[:], in0=c2[:], scalar1=-1.0, scalar2=None, op0=ALU.add), reads=["csum"], writes=["csum"])
            P.dma("sp", slot_s, c2[:], reads=["csum"], writes=["scr_slot"])
            slotT = gl["slotT"]
            for tt in range(NT):
                pt, pk = PS(tt % 4)
                P.op("pe", lambda e, pt=pt, tt=tt: e.transpose(pt[:, 0:E], c2[:, tt * 128:(tt + 1) * 128], ident[0:E, 0:E]), reads=["csum", "ident"], writes=[pk])
                P.op("dve", lambda e, pt=pt, tt=tt: e.tensor_copy(out=slotT[:, tt, :], in_=pt[:, 0:E]), reads=[pk], writes=["slotT"])
            P.barrier()

    def stage_experts(layer, gl):
        U32 = mybir.dt.uint32
        with ExitStack() as stk:
            slotT = gl["slotT"]; aff_tok = gl["aff_tok"]
            iota = k.sb(stk, "iota", [128, 512], F32)
            P.dma("sp", iota[:], iota_d.partition_broadcast(128), writes=["iota"])
            aux2 = [k.sb(stk, "aux%d" % i, [128, NT, 4], BF16) for i in range(2)]
            for i in range(2):
                P.dma("pool", aux2[i][:, :, 0:2], auxc_d, writes=["aux01"])
            hif2 = [k.sb(stk, "hif%d" % i, [128, NT], F32) for i in range(2)]
            pis2 = [k.sb(stk, "pis%d" % i, [128, 16], F32) for i in range(2)]
            Pm = k.sb(stk, "Pm", [128, NT, 512], BF16)
            xs = [k.sb(stk, "xs%d" % i, [128, D], BF16) for i in range(4)]
            xsT = k.sb(stk, "xsT", [128, 16, 512], BF16)
            hT = k.sb(stk, "hT", [128, 16, 512], BF16)
            idxf2 = [k.sb(stk, "idxf%d" % i, [128, 4], F32) for i in range(2)]; idxi2 = [k.sb(stk, "idxi%d" % i, [128, 4], U32) for i in range(2)]
            gate2 = [k.sb(stk, "gate%d" % i, [128, 4], F32) for i in range(2)]
            w1t = [k.sb(stk, "w1t%d" % i, [128, 16, 256], BF16) for i in range(2)]
            w3t = [k.sb(stk, "w3t%d" % i, [128, 16, 256], BF16) for i in range(2)]
            w2b = k.sb(stk, "w2b", [128, 16, D], BF16)
            sl = [k.sb(stk, "sl%d" % i, [128, 512], F32) for i in range(2)]
            yo = [k.sb(stk, "yo%d" % i, [128, D], F32) for i in range(2)]
            P.op("dve", lambda e: e.memset(yo[0][:], 0.0), writes=["yo0_%d" % nb for nb in range(4)])
            for tt in range(NT):
                P.dma(k.q(), moe_s[tt * 128:(tt + 1) * 128, :], yo[0][:], reads=["yo0_%d" % nb for nb in range(4)], writes=["moe_acc"])
            def pro_a(ex):
                par = ex % 2
                for tt in range(NT):
                    eng = "dve" if tt % 2 == 0 else "pool"
                    P.op(eng, lambda e, tt=tt, ex=ex: e.tensor_scalar(out=Pm[:, tt, :], in0=iota[:], scalar1=slotT[:, tt, ex:ex + 1], scalar2=None, op0=ALU.is_equal),
                         reads=["iota", "slotT"], writes=["Pm%d" % tt])
                ax = aux2[par]
                P.op("dve", lambda e, ex=ex, ax=ax: e.tensor_copy(out=ax[:, :, 2], in_=aff_tok[:, :, ex]), reads=["aff_tok"], writes=["aux2_%d" % par])
                P.op("dve", lambda e, ax=ax, par=par: e.tensor_copy(out=hif2[par][:], in_=ax[:, :, 2]), reads=["aux2_%d" % par], writes=["hif%d" % par])
                P.op("dve", lambda e, ex=ex, ax=ax, par=par: e.tensor_tensor(out=ax[:, :, 3], in0=aff_tok[:, :, ex], in1=hif2[par][:], op=ALU.subtract),
                     reads=["aff_tok", "hif%d" % par], writes=["aux3_%d" % par])

            def pro_b(ex):
                par = ex % 2
                ax = aux2[par]
                pi, pik = PS(6)
                for sc in range(4):
                    for tt in range(NT):
                        P.op("pe", lambda e, sc=sc, tt=tt, ax=ax: e.matmul(pi[:, sc * 4:(sc + 1) * 4], Pm[:, tt, sc * 128:(sc + 1) * 128], ax[:, tt, :], start=(tt == 0), stop=(tt == NT - 1)),
                             reads=["Pm%d" % tt, "aux01", "aux2_%d" % par, "aux3_%d" % par], writes=[pik])
                pis = pis2[par]; idxf = idxf2[par]; idxi = idxi2[par]; gate = gate2[par]
                P.op("dve", lambda e, pis=pis: e.tensor_copy(out=pis[:], in_=pi[:, 0:16]), reads=[pik], writes=["pis%d" % par])
                for sc in range(4):
                    P.op("dve", lambda e, sc=sc, pis=pis, idxf=idxf: e.scalar_tensor_tensor(out=idxf[:, sc:sc + 1], in0=pis[:, sc * 4 + 1:sc * 4 + 2], scalar=128.0, in1=pis[:, sc * 4:sc * 4 + 1],
                                                                                       op0=ALU.mult, op1=ALU.add), reads=["pis%d" % par], writes=["idxf%d" % par])
                    P.op("dve", lambda e, sc=sc, pis=pis, gate=gate: e.tensor_tensor(out=gate[:, sc:sc + 1], in0=pis[:, sc * 4 + 2:sc * 4 + 3], in1=pis[:, sc * 4 + 3:sc * 4 + 4], op=ALU.add),
                         reads=["pis%d" % par], writes=["gate%d" % par])
                P.op("dve", lambda e, idxf=idxf, idxi=idxi: e.tensor_copy(out=idxi[:], in_=idxf[:]), reads=["idxf%d" % par], writes=["idxi%d" % par])
                for sc in range(4):
                    xx = xs[sc]; xk = "xs%d" % sc
                    P.idma(lambda e, xx=xx, sc=sc, idxi=idxi: e.indirect_dma_start(out=xx[:], out_offset=None, in_=x1b_s,
                                                                                  in_offset=bass.IndirectOffsetOnAxis(ap=idxi[:, sc:sc + 1], axis=0)),
                           reads=["idxi%d" % par], writes=[xk])

            def main1(ex):
                for kc in range(16):
                    P.dma("pool", w2b[:, kc, :], ew2_d[layer, ex, kc * 128:(kc + 1) * 128, :], writes=["w2b"])
                for sc in range(4):
                    xx = xs[sc]; xk = "xs%d" % sc
                    for half in range(2):
                        for j in range(8):
                            kc = half * 8 + j
                            P.op("pe", lambda e, xx=xx, kc=kc, j=j: e.transpose(psb[:, j * 128:(j + 1) * 128], xx[:, kc * 128:(kc + 1) * 128], identb[:]),
                                 reads=[xk, "identb"], writes=["psb"])
                        dst = xsT[:, half * 8:(half + 1) * 8, sc * 128:(sc + 1) * 128]
                        src = psb[:, :].rearrange("p (a b) -> p a b", a=8)
                        wk = ["xsT%d" % kc for kc in range(half * 8, half * 8 + 8)]
                        if half == 0:
                            P.op("act", lambda e, dst=dst, src=src: e.copy(out=dst, in_=src), reads=["psb"], writes=wk)
                        else:
                            P.op("dve", lambda e, dst=dst, src=src: e.tensor_copy(out=dst, in_=src), reads=["psb"], writes=wk)
                for fc in range(16):
                    fq = fc // 2
                    wa = w1t[fq % 2]; wb = w3t[fq % 2]; wak = "w1t%d" % (fq % 2); wbk = "w3t%d" % (fq % 2)
                    if fc % 2 == 0:
                        P.dma("pool", wa[:], ew1_d[layer, ex, :, fq * 256:(fq + 1) * 256].rearrange("(kc p) m -> p kc m", p=128), writes=[wak])
                        P.dma("pool", wb[:], ew3_d[layer, ex, :, fq * 256:(fq + 1) * 256].rearrange("(kc p) m -> p kc m", p=128), writes=[wbk])
                    fo = (fc % 2) * 128
                    pa, pak = PS(4 + fc % 2); pb, pbk = PS(0 + fc % 2)
                    for kc in range(16):
                        P.op("pe", lambda e, pa=pa, wa=wa, kc=kc, fo=fo: e.matmul(pa[:], wa[:, kc, fo:fo + 128], xsT[:, kc, :], start=(kc == 0), stop=(kc == 15)),
                             reads=[wak, "xsT%d" % kc], writes=[pak])
                    for kc in range(16):
                        P.op("pe", lambda e, pb=pb, wb=wb, kc=kc, fo=fo: e.matmul(pb[:], wb[:, kc, fo:fo + 128], xsT[:, kc, :], start=(kc == 0), stop=(kc == 15)),
                             reads=[wbk, "xsT%d" % kc], writes=[pbk])
                    s_ = sl[fc % 2]; sk = "sl%d" % (fc % 2)
                    P.op("act", lambda e, pa=pa, s_=s_: e.activation(out=s_[:], in_=pa[:], func=AF.Silu), reads=[pak], writes=[sk])
                    P.op("dve", lambda e, pb=pb, s_=s_, fc=fc: e.tensor_tensor(out=hT[:, fc, :], in0=s_[:], in1=pb[:], op=ALU.mult), reads=[sk, pbk], writes=["hT%d" % fc])

            def down(ex, scs):
                par = ex % 2
                idxi = idxi2[par]; gate = gate2[par]
                for sc in scs:
                    yy = yo[sc % 2]; yk = "yo%d" % (sc % 2)
                    for nb in range(4):
                        pt, pk = PS(nb)
                        for fc in range(16):
                            P.op("pe", lambda e, pt=pt, fc=fc, sc=sc, nb=nb: e.matmul(pt[:], hT[:, fc, sc * 128:(sc + 1) * 128], w2b[:, fc, nb * 512:(nb + 1) * 512],
                                                                                   start=(fc == 0), stop=(fc == 15)), reads=["hT%d" % fc, "w2b"], writes=[pk])
                        if nb % 2 == 0:
                            P.op("act", lambda e, pt=pt, yy=yy, nb=nb, sc=sc, gate=gate: e.activation(out=yy[:, nb * 512:(nb + 1) * 512], in_=pt[:], func=AF.Identity, scale=gate[:, sc:sc + 1]),
                                 reads=[pk, "gate%d" % par], writes=[yk + "_%d" % nb])
                        else:
                            P.op("dve", lambda e, pt=pt, yy=yy, nb=nb, sc=sc, gate=gate: e.tensor_scalar(out=yy[:, nb * 512:(nb + 1) * 512], in0=pt[:], scalar1=gate[:, sc:sc + 1], scalar2=None, op0=ALU.mult),
                                 reads=[pk, "gate%d" % par], writes=[yk + "_%d" % nb])
                    P.idma(lambda e, yy=yy, sc=sc, idxi=idxi: e.indirect_dma_start(out=moe_s, out_offset=bass.IndirectOffsetOnAxis(ap=idxi[:, sc:sc + 1], axis=0), in_=yy[:], in_offset=None,
                                                                                  compute_op=ALU.add),
                           reads=["idxi%d" % par, "moe_acc"] + [yk + "_%d" % nb for nb in range(4)], writes=["moe_acc"])

            pro_a(0)
            pro_b(0)
            for ex in range(E):
                main1(ex)
                if ex + 1 < E:
                    pro_a(ex + 1)
                down(ex, (0, 1))
                if ex + 1 < E:
                    pro_b(ex + 1)
                down(ex, (2, 3))
            P.barrier()

    def stage_return():
        with ExitStack() as stk:
            pidx = k.sb(stk, "pidx", [128, 4], F32)
            P.dma("sp", pidx[:], pidx_d, writes=["pidx"])
            yall = k.sb(stk, "yall", [128, 64, 512], BF16)
            sbc = [k.sb(stk, "sbc%d" % i, [128, E, 128], F32) for i in range(2)]
            abc = [k.sb(stk, "abc%d" % i, [128, E, 128], F32) for i in range(2)]
            pg = [k.sb(stk, "pg%d" % i, [128, 64, 128], BF16) for i in range(2)]
            mo = [k.sb(stk, "mo%d" % i, [128, 512], F32) for i in range(2)]
            for nb in range(4):
                for ec in range(64):
                    P.dma(k.q(), yall[:, ec, :], y_s[ec * 128:(ec + 1) * 128, nb * 512:(nb + 1) * 512], writes=["yall"])
                for tt in range(NT):
                    b = tt % 2
                    P.dma("sp", sbc[b][:], slot_s[:, tt * 128:(tt + 1) * 128].partition_broadcast(128), writes=["sbc%d" % b])
                    P.dma("act", abc[b][:], affT_s[:, tt * 128:(tt + 1) * 128].partition_broadcast(128), writes=["abc%d" % b])
                    for ex in range(E):
                        for sc in range(4):
                            eng = "dve"
                            P.op(eng, lambda e, ex=ex, sc=sc, b=b: e.scalar_tensor_tensor(out=pg[b][:, ex * 4 + sc, :], in0=sbc[b][:, ex, :], scalar=pidx[:, sc:sc + 1],
                                                                                         in1=abc[b][:, ex, :], op0=ALU.is_equal, op1=ALU.mult),
                                 reads=["sbc%d" % b, "abc%d" % b, "pidx"], writes=["pg%d_%d" % (b, ex * 4 + sc)])
                    pt, pk = PS(tt % 4)
                    for ec in range(64):
                        P.op("pe", lambda e, pt=pt, ec=ec, b=b: e.matmul(pt[:], pg[b][:, ec, :], yall[:, ec, :], start=(ec == 0), stop=(ec == 63)),
                             reads=["pg%d_%d" % (b, ec), "yall"], writes=[pk])
                    P.op("act", lambda e, pt=pt, b=b: e.copy(out=mo[b][:], in_=pt[:]), reads=[pk], writes=["mo%d" % b])
                    P.dma("sp", moe_s[tt * 128:(tt + 1) * 128, nb * 512:(nb + 1) * 512], mo[b][:], reads=["mo%d" % b], writes=["scr_moe"])
            P.barrier()

    def stage_ffn_out(layer, final):
        with ExitStack() as stk:
            lt = ln_tiles(stk, layer * 4 + 2)
            xr = [k.sb(stk, "fxr%d" % i, [128, D], F32) for i in range(2)]
            mr = [k.sb(stk, "fmr%d" % i, [128, D], F32) for i in range(2)]
            fres2 = [k.sb(stk, "fres%d" % i, [128, D], F32) for i in range(2)]
            y = [k.sb(stk, "fy%d" % i, [128, D], F32) for i in range(2)]
            yT = [k.sb(stk, "fyT%d" % i, [128, 16, 128], F32) for i in range(2)]
            for tt in range(NT):
                b = tt % 2
                P.dma("act", xr[b][:], x1_s[tt * 128:(tt + 1) * 128, :], writes=["fxr%d" % b])
                P.dma("sp", mr[b][:], moe_s[tt * 128:(tt + 1) * 128, :], writes=["fmr%d" % b])
                res = fres2[b]
                P.op("dve", lambda e, b=b, res=res: e.scalar_tensor_tensor(out=res[:], in0=xr[b][:], scalar=ALPHA, in1=mr[b][:], op0=ALU.mult, op1=ALU.add),
                     reads=["fxr%d" % b, "fmr%d" % b], writes=["fres%d" % b])
                layer_norm(lt, res, "fres%d" % b, b, "fy%d" % b, y[b])
                if final:
                    P.dma("sp", out_d[tt * 128:(tt + 1) * 128, :], y[b][:], reads=["fy%d" % b], writes=["out"], is_output=True)
                else:
                    P.dma("sp", x2_s[tt * 128:(tt + 1) * 128, :], y[b][:], reads=["fy%d" % b], writes=["scr_x2"])
                    for kc in range(16):
                        pt, pk = PS(kc % 4)
                        P.op("pe", lambda e, pt=pt, kc=kc, b=b: e.transpose(pt[:, 0:128], y[b][:, kc * 128:(kc + 1) * 128], ident[:]), reads=["fy%d" % b, "ident"], writes=[pk])
                        if kc % 2 == 0:
                            P.op("act", lambda e, pt=pt, kc=kc, b=b: e.copy(out=yT[b][:, kc, :], in_=pt[:, 0:128]), reads=[pk], writes=["fyT%d_%d" % (b, kc)])
                        else:
                            P.op("dve", lambda e, pt=pt, kc=kc, b=b: e.tensor_copy(out=yT[b][:, kc, :], in_=pt[:, 0:128]), reads=[pk], writes=["fyT%d_%d" % (b, kc)])
                    P.dma("act", x2T_s[:, tt * 128:(tt + 1) * 128].rearrange("(kc p) t -> p kc t", p=128), yT[b][:],
                          reads=["fyT%d_%d" % (b, kc) for kc in range(16)], writes=["scr_x2T"])
            P.barrier()

    def stage_pool():
        with ExitStack() as stk:
            xT = [k.sb(stk, "pxT%d" % i, [128, S], F32) for i in range(4)]
            acc = k.sb(stk, "pacc", [128, S], F32)
            dT = k.sb(stk, "pdT", [128, 4, S], BF16)
            pinv = k.sb(stk, "pinv", [128, S], F32)
            pw = k.sb(stk, "ppw", [128, 4, 512], BF16)
            psc = k.sb(stk, "ppsc", [128, 512], F32)
            po = [k.sb(stk, "ppo%d" % i, [128, 512], F32) for i in range(2)]
            for gi, win in enumerate((2, 4, 8, 16)):
                hw = win // 2
                P.dma("sp", pinv[:], pinv_d[gi:gi + 1, :].partition_broadcast(128), writes=["pinv"])
                P.dma("sp", psc[:], pscale_d[0:1, gi * 512:(gi + 1) * 512].partition_broadcast(128), writes=["ppsc"])
                for kc in range(4):
                    P.dma("pool", pw[:, kc, :], poolw_d[gi, kc * 128:(kc + 1) * 128, :], writes=["ppw"])
                for ci in range(4):
                    x_ = xT[ci]; xk = "pxT%d" % ci
                    P.dma(k.q(), x_[:], x2T_s[gi * 512 + ci * 128: gi * 512 + (ci + 1) * 128, :], writes=[xk])
                    P.op("dve", lambda e, x_=x_: e.tensor_copy(out=acc[:], in_=x_[:]), reads=[xk], writes=["pacc"])
                    for s_ in range(-hw, hw):
                        if s_ == 0:
                            continue
                        if s_ < 0:
                            a = -s_
                            P.op("dve", lambda e, x_=x_, a=a: e.tensor_tensor(out=acc[:, a:S], in0=acc[:, a:S], in1=x_[:, 0:S - a], op=ALU.add), reads=[xk, "pacc"], writes=["pacc"])
                        else:
                            a = s_
                            P.op("dve", lambda e, x_=x_, a=a: e.tensor_tensor(out=acc[:, 0:S - a], in0=acc[:, 0:S - a], in1=x_[:, a:S], op=ALU.add), reads=[xk, "pacc"], writes=["pacc"])
                    P.op("dve", lambda e: e.tensor_tensor(out=acc[:], in0=acc[:], in1=pinv[:], op=ALU.mult), reads=["pacc", "pinv"], writes=["pacc"])
                    P.op("dve", lambda e, x_=x_, ci=ci: e.tensor_tensor(out=dT[:, ci, :], in0=acc[:], in1=x_[:], op=ALU.subtract), reads=["pacc", xk], writes=["pdT%d" % ci])
                for tt in range(NT):
                    pt, pk = PS(tt % 4)
                    for ci in range(4):
                        P.op("pe", lambda e, pt=pt, ci=ci, tt=tt: e.matmul(pt[:], dT[:, ci, tt * 128:(tt + 1) * 128], pw[:, ci, :], start=(ci == 0), stop=(ci == 3)),
                             reads=["pdT%d" % ci, "ppw"], writes=[pk])
                    o = po[tt % 2]; ok = "ppo%d" % (tt % 2)
                    P.op("dve", lambda e, pt=pt, o=o: e.tensor_tensor(out=o[:], in0=pt[:], in1=psc[:], op=ALU.mult), reads=[pk, "ppsc"], writes=[ok])
                    P.dma("sp", pl_s[tt * 128:(tt + 1) * 128, gi * 512:(gi + 1) * 512], o[:], reads=[ok], writes=["scr_pl"])
            P.barrier()

    gl = {}
    gl["aff_tok"] = k.sb(g, "aff_tok", [128, NT, E], F32)
    gl["slotT"] = k.sb(g, "slotT", [128, NT, E], F32)
    if on("proj"):
        stage_proj()
    if on("attn"):
        stage_attn()
    if on("hyena"):
        stage_hyena()
    for layer in range(2):
        if on("mix%d" % layer):
            if layer == 1 and on("pool"):
                stage_pool()
            stage_mix_out(layer, gl, mixer_attnhy if layer == 0 else mixer_pool)
        if on("sel%d" % layer):
            stage_select(gl)
        if on("exp%d" % layer):
            stage_experts(layer, gl)
        if on("ffn%d" % layer):
            stage_ffn_out(layer, final=(layer == 1))
    P.finish()
    k.st.close()
    return k


def _consts():
    c = {}
    L = S
    t = np.linspace(0.0, 1.0, L, dtype=np.float32)[:, None]
    bands = 16
    w_ang = (2.0 * math.pi * np.arange(L, dtype=np.float32)[:, None] / L).astype(np.float32)
    f = np.linspace(1e-4, bands - 1, bands, dtype=np.float32)[None, :]
    z = np.concatenate([t, np.cos(f * w_ang), -np.sin(f * w_ang)], axis=-1).astype(np.float32)
    c["zf"] = np.ascontiguousarray(z.T)
    c["tlin"] = np.ascontiguousarray(t.T)
    max_decay = math.log(1e-2) / 0.3
    min_decay = math.log(1e-2) / 1.5
    deltas = np.linspace(min_decay, max_decay, HW, dtype=np.float32)
    c["ndelta"] = (-np.abs(deltas)).reshape(HW, 1).astype(np.float32)
    slopes = (2.0 ** (-(8.0 / 8) * np.arange(1, 9, dtype=np.float32))).astype(np.float32)
    a = np.arange(128)[:, None]
    cq = np.arange(256)[None, :]
    rel = a + 64 - cq
    ab = np.zeros((128, 8, 3, 256), np.float32)
    for h in range(8):
        for di, (_, d) in enumerate(DIL):
            ab[:, h, di, :] = np.where(np.abs(rel) <= 64, -slopes[h] * np.abs(rel) * d, -1e30)
    c["abias"] = ab.reshape(128, 8 * 3 * 256)
    c["iota"] = np.arange(512, dtype=np.float32)[None, :]
    c["pidx"] = (np.arange(128)[:, None] + 128 * np.arange(4)[None, :]).astype(np.float32)
    c["ident"] = np.eye(128, dtype=np.float32)
    pos = np.arange(S)
    pinv = np.zeros((4, S), np.float32)
    for gi, win in enumerate((2, 4, 8, 16)):
        lo = np.clip(pos - win // 2, 0, S)
        hi = np.clip(pos + win // 2, 0, S)
        pinv[gi] = 1.0 / (hi - lo).astype(np.float32)
    c["pinv"] = pinv
    auxc = np.zeros((128, NT, 2), np.float32)
    auxc[:, :, 0] = np.arange(128)[:, None]
    auxc[:, :, 1] = np.arange(NT)[None, :]
    c["auxc"] = auxc
    c["ndrow"] = c["ndelta"].reshape(1, HW).copy()
    lag = np.arange(128)[:, None] + 128 * np.arange(NT)[None, :]
    c["tcol"] = (lag.astype(np.float64) / (S - 1)).astype(np.float32)
    m0 = np.ones((128, NT), np.float32); m0[0, 0] = 0.0
    c["m0col"] = m0
    fidx = np.arange(128)[:, None] + 128 * np.arange(33)[None, :]
    cf = np.where((fidx == 0) | (fidx == S), 1.0 / (2 * S), np.where(fidx < S, 2.0 / (2 * S), 0.0))
    c["coef"] = cf.astype(np.float32)
    n = 4224
    ii = np.arange(n, dtype=np.int64)
    prod = (ii[:, None] * ii[None, :]) % (2 * S)
    ang = prod.astype(np.float64) * (2.0 * math.pi / (2 * S))
    valid = (ii[:, None] <= S) & (ii[None, :] <= S)
    for nm, fnc in (("c", np.cos), ("s", np.sin)):
        tab = np.where(valid, fnc(ang), 0.0).astype(np.float32).astype(ml_dtypes.bfloat16)
        c["dft%sF" % nm] = np.ascontiguousarray(tab[:S, :].reshape(NT, 128, 33, 128).transpose(2, 1, 0, 3).reshape(33, 128, NT * 128))
        c["dft%sI" % nm] = np.ascontiguousarray(tab[:, :S].reshape(33, 128, 16, 256).transpose(2, 1, 0, 3).reshape(16, 128, 33 * 256))
    return c


def make_in_maps(x, mix_w_in, mix_w_out, hy_conv_w, hy_conv_b, hy_ffn_w1, hy_ffn_b1, hy_ffn_w2,
                 hy_ffn_b2, hy_ffn_w3, hy_ffn_b3, hy_ffn_w4, hy_sin_freq, hy_bias, pool_w, pool_scale,
                 ln_mix_g, ln_mix_b, ln_ffn_g, ln_ffn_b, router_w, exp_w1, exp_w3, exp_w2, ncore=NCORE):
    f = lambda a: np.ascontiguousarray(np.asarray(a, dtype=np.float32))
    shared = dict(
        w_in=f(mix_w_in[0]), w_out=f(mix_w_out[0]), convw=f(np.asarray(hy_conv_w)[0].T), convb=f(np.asarray(hy_conv_b)[0].reshape(3072, 1)),
        fw1=f(hy_ffn_w1[0]), fw2=f(hy_ffn_w2[0]), fw3=f(hy_ffn_w3[0]), fw4=f(hy_ffn_w4[0]),
        fvec=f(np.stack([np.asarray(hy_ffn_b1)[0], np.asarray(hy_ffn_b2)[0], np.asarray(hy_ffn_b3)[0], np.asarray(hy_sin_freq)[0]], axis=1)),
        hyb=f(np.asarray(hy_bias)[0].reshape(HW, 1)), poolw=f(pool_w[0]), pscale=f(np.asarray(pool_scale)[0].reshape(1, D)),
        ln=f(np.stack([np.asarray(ln_mix_g)[0], np.asarray(ln_mix_b)[0], np.asarray(ln_ffn_g)[0], np.asarray(ln_ffn_b)[0],
                       np.asarray(ln_mix_g)[1], np.asarray(ln_mix_b)[1], np.asarray(ln_ffn_g)[1], np.asarray(ln_ffn_b)[1]], axis=0)),
        rw=f(router_w), ew1=f(exp_w1), ew3=f(exp_w3), ew2=f(exp_w2),
    )
    shared.update(_consts())
    maps = []
    xa = np.asarray(x, dtype=np.float32)
    for c in range(ncore):
        b = c // 2
        m = dict(shared)
        m["x"] = np.ascontiguousarray(xa[b])
        m["xT"] = np.ascontiguousarray(xa[b].T)
        maps.append(m)
    return maps


_CACHE = {}


def kernel(**inputs):
    if "k" not in _CACHE:
        _CACHE["k"] = build()
    k = _CACHE["k"]
    maps = make_in_maps(**inputs)
    res = run_bass_kernel_spmd(k.nc, maps, core_ids=list(range(NCORE)))
    out = np.empty((4, S, D), np.float32)
    for b in range(4):
        out[b, :S // 2] = res.results[2 * b]["out"][:S // 2]
        out[b, S // 2:] = res.results[2 * b + 1]["out"][S // 2:]
    return out
```

```python
import math
import numpy as np
import ml_dtypes
from contextlib import ExitStack
import concourse.bass as bass
import concourse.mybir as mybir
from concourse.bass_utils import run_bass_kernel_spmd

F32 = mybir.dt.float32
BF16 = mybir.dt.bfloat16
ALU = mybir.AluOpType
AF = mybir.ActivationFunctionType
AX = mybir.AxisListType

S = 4096
D = 2048
NT = S // 128
E = 16
CAP = 2 * S // E
FF = 2048
HW = 1024
ALPHA = 4 ** 0.25
EPS = 1e-5
NCORE = 8
DIL = ((128, 1), (512, 4), (2048, 16))
NQ_DBG = 4


class Prog:
    ENGS = ("pe", "dve", "act", "pool", "sp")
    NDMA = 6

    def __init__(self, nc, stack):
        self.nc = nc
        self.rec = {e: [] for e in self.ENGS}
        self.sem = {e: stack.enter_context(nc.semaphore("s_" + e)) for e in self.ENGS}
        self.cnt = {e: 0 for e in self.ENGS}
        self.dsem = {e: [stack.enter_context(nc.semaphore("d_%s%d" % (e, i))) for i in range(self.NDMA)]
                     for e in ("sp", "act", "pool")}
        self.dval = {e: [0] * self.NDMA for e in self.dsem}
        self.drr = {e: 0 for e in self.dsem}
        self.known = {e: {} for e in self.ENGS}
        self.lastw = {}
        self.readers = {}
        self.final = []

    def _deps(self, reads, writes):
        deps = []
        for k in list(reads) + list(writes):
            deps.extend(self.lastw.get(k, ()))
        for k in writes:
            deps.extend(self.readers.get(k, ()))
        return deps

    def _commit(self, tok, reads, writes):
        for k in reads:
            self.readers.setdefault(k, []).append(tok)
        for k in writes:
            if self.readers.get(k):
                self.lastw[k] = [tok]
            else:
                lst = self.lastw.setdefault(k, [])
                lst.append(tok)
                if len(lst) > 64:
                    best = {}
                    for t in lst:
                        if t[0].name not in best or best[t[0].name][1] < t[1]:
                            best[t[0].name] = t
                    self.lastw[k] = list(best.values())
            self.readers[k] = []

    def _waits(self, eng, deps):
        need = {}
        for (sem, val, src) in deps:
            if src == eng and eng == "pe":
                continue
            key = sem.name
            if self.known[eng].get(key, 0) >= val:
                continue
            if key not in need or need[key][1] < val:
                need[key] = (sem, val)
        for key, (sem, val) in need.items():
            self.known[eng][key] = val
        return list(need.values())

    def op(self, eng, fn, reads=(), writes=()):
        waits = self._waits(eng, self._deps(reads, writes))
        self.cnt[eng] += 1
        sem = self.sem[eng]

        def emit(e, fn=fn, waits=waits, sem=sem):
            for (s, v) in waits:
                e.wait_ge(s, v)
            fn(e).then_inc(sem, 1)
        self.rec[eng].append(emit)
        self._commit((sem, self.cnt[eng], eng), reads, writes)

    def dma(self, eng, out, in_, reads=(), writes=(), is_output=False, **kw):
        deps = self._deps(reads, writes)
        i = self.drr[eng]
        self.drr[eng] = (i + 1) % self.NDMA
        dsem = self.dsem[eng][i]
        prev = self.dval[eng][i]
        if prev > 0:
            deps.append((dsem, prev, "dma"))
        waits = self._waits(eng, deps)
        self.dval[eng][i] = prev + 16

        def emit(e, waits=waits, dsem=dsem, out=out, in_=in_, kw=kw):
            for (s, v) in waits:
                e.wait_ge(s, v)
            e.dma_start(out=out, in_=in_, **kw).then_inc(dsem, 16)
        self.rec[eng].append(emit)
        tok = (dsem, prev + 16, "dma")
        self._commit(tok, reads, writes)
        if is_output:
            self.final.append(tok)

    def idma(self, fn, reads=(), writes=()):
        eng = "pool"
        deps = self._deps(reads, writes)
        i = self.drr[eng]
        self.drr[eng] = (i + 1) % self.NDMA
        dsem = self.dsem[eng][i]
        prev = self.dval[eng][i]
        if prev > 0:
            deps.append((dsem, prev, "dma"))
        waits = self._waits(eng, deps)
        self.dval[eng][i] = prev + 16

        def emit(e, waits=waits, dsem=dsem, fn=fn):
            for (s, v) in waits:
                e.wait_ge(s, v)
            fn(e).then_inc(dsem, 16)
        self.rec[eng].append(emit)
        self._commit((dsem, prev + 16, "dma"), reads, writes)

    def _all_tokens(self):
        toks = []
        for e in self.dsem:
            for i in range(self.NDMA):
                if self.dval[e][i] > 0:
                    toks.append((self.dsem[e][i], self.dval[e][i], "dma"))
        for e in self.ENGS:
            if self.cnt[e] > 0:
                toks.append((self.sem[e], self.cnt[e], "x"))
        return toks

    def barrier(self):
        toks = self._all_tokens()
        for eng in self.ENGS:
            waits = self._waits(eng, toks)

            def emit(e, waits=waits):
                for (s, v) in waits:
                    e.wait_ge(s, v)
            self.rec[eng].append(emit)
        self.lastw = {}
        self.readers = {}

    def finish(self):
        self.barrier()
        nc = self.nc
        with nc.Block() as block:
            @block.tensor
            def _(e):
                for f in self.rec["pe"]:
                    f(e)

            @block.vector
            def _(e):
                for f in self.rec["dve"]:
                    f(e)

            @block.scalar
            def _(e):
                for f in self.rec["act"]:
                    f(e)

            @block.gpsimd
            def _(e):
                for f in self.rec["pool"]:
                    f(e)

            @block.sync
            def _(e):
                for f in self.rec["sp"]:
                    f(e)


class K:
    def __init__(self, dbg=None):
        self.dbg = dbg
        nc = self.nc = bass.Bass("TRN2", target_bir_lowering=False)
        self.inp = {}
        self.st = ExitStack()
        self.P = Prog(nc, self.st)
        self.rrq = 0

    def din(self, name, shape, dt=F32):
        ap = self.nc.dram_tensor(name, list(shape), dt, kind="ExternalInput").ap()
        self.inp[name] = ap
        return ap

    def dscr(self, name, shape, dt=F32):
        if self.dbg and name in self.dbg:
            return self.nc.dram_tensor(name, list(shape), dt, kind="ExternalOutput").ap()
        return self.nc.dram_tensor(name, list(shape), dt).ap()

    def sb(self, stack, name, shape, dt=F32):
        self.uid = getattr(self, "uid", 0) + 1
        return stack.enter_context(self.nc.sbuf_tensor("sb%d_%s" % (self.uid, name), list(shape), dt))

    def q(self):
        self.rrq += 1
        return ("sp", "act")[self.rrq % 2]


def build(dbg=None, stages=None):
    k = K(dbg)
    nc, P = k.nc, k.P
    x_d = k.din("x", [S, D])
    xT_d = k.din("xT", [D, S])
    w_in_d = k.din("w_in", [D, 6144])
    w_out_d = k.din("w_out", [D, D])
    convw_d = k.din("convw", [3072, 3])
    convb_d = k.din("convb", [3072, 1])
    fw1_d = k.din("fw1", [33, 64]); fw2_d = k.din("fw2", [64, 64]); fw3_d = k.din("fw3", [64, 64])
    fw4_d = k.din("fw4", [64, 2048])
    fvec_d = k.din("fvec", [64, 4])
    hyb_d = k.din("hyb", [HW, 1])
    poolw_d = k.din("poolw", [4, 512, 512])
    pscale_d = k.din("pscale", [1, D])
    ln_d = k.din("ln", [8, D])
    rw_d = k.din("rw", [2, D, E])
    ew1_d = k.din("ew1", [2, E, D, FF]); ew3_d = k.din("ew3", [2, E, D, FF]); ew2_d = k.din("ew2", [2, E, FF, D])
    zf_d = k.din("zf", [33, S])
    tlin_d = k.din("tlin", [1, S])
    ndelta_d = k.din("ndelta", [HW, 1])
    abias_d = k.din("abias", [128, 8 * 3 * 256])
    iota_d = k.din("iota", [1, 512])
    pidx_d = k.din("pidx", [128, 4])
    ident_d = k.din("ident", [128, 128])
    pinv_d = k.din("pinv", [4, S])
    auxc_d = k.din("auxc", [128, NT, 2])
    ndrow_d = k.din("ndrow", [1, HW]); tcol_d = k.din("tcol", [128, NT]); m0col_d = k.din("m0col", [128, NT]); coef_d = k.din("coef", [128, 33])
    dftcF_d = k.din("dftcF", [33, 128, NT * 128], BF16); dftsF_d = k.din("dftsF", [33, 128, NT * 128], BF16)
    dftcI_d = k.din("dftcI", [16, 128, 33 * 256], BF16); dftsI_d = k.din("dftsI", [16, 128, 33 * 256], BF16)
    out_d = nc.dram_tensor("out", [S, D], F32, kind="ExternalOutput").ap()
    qk_s = k.dscr("qk_s", [3072, S], BF16)
    hy_s = k.dscr("hy_s", [3072, S], F32)
    mixT_s = k.dscr("mixT_s", [D, S], BF16)
    x1_s = k.dscr("x1_s", [S, D], F32)
    x1b_s = k.dscr("x1b_s", [S, D], BF16)
    slot_s = k.dscr("slot_s", [E, S], F32)
    affT_s = k.dscr("affT_s", [E, S], F32)
    y_s = k.dscr("y_s", [E * CAP, D], BF16)
    moe_s = k.dscr("moe_s", [S, D], F32)
    x2_s = k.dscr("x2_s", [S, D], F32)
    x2T_s = k.dscr("x2T_s", [D, S], F32)
    pl_s = k.dscr("pl_s", [S, D], F32)
    hs_s = k.dscr("hs_s", [S, HW], BF16); hd_s = k.dscr("hd_s", [S, HW], BF16); ztok_s = k.dscr("ztok_s", [S, HW], BF16)
    x0_s = k.dscr("x0_s", [HW, S], F32); z_s = k.dscr("z_s", [HW, S], F32)

    g = k.st
    ps = [g.enter_context(nc.psum_tensor("ps%d" % i, [128, 512], F32)) for i in range(7)]
    psb = g.enter_context(nc.psum_tensor("psb", [128, 1024], BF16))
    ident = k.sb(g, "ident", [128, 128], F32)
    identb = k.sb(g, "identb", [128, 128], BF16)
    onesb = k.sb(g, "onesb", [128, 128], BF16)
    P.dma("sp", ident[:], ident_d, writes=["ident"])
    P.dma("pool", identb[:], ident_d, writes=["identb"])
    P.op("dve", lambda e: e.memset(onesb[:], 1.0), writes=["onesb"])
    P.barrier()

    def PS(i):
        return ps[i], "ps%d" % i

    def on(s):
        return stages is None or s in stages

    def layer_norm(st, res, reskey, par, outkey, y):
        stats, mv, rstd, nmr = st["stats"][par], st["mv"][par], st["rstd"][par], st["nmr"][par]
        sk = "_%d" % par
        for c in range(4):
            P.op("dve", lambda e, c=c: e.bn_stats(stats[:, c * 6:(c + 1) * 6], res[:, c * 512:(c + 1) * 512]),
                 reads=[reskey], writes=["stats%d" % c + sk])
        P.op("dve", lambda e: e.bn_aggr(mv[:], stats[:]), reads=["stats%d" % c + sk for c in range(4)], writes=["mv" + sk])
        P.op("dve", lambda e: e.tensor_scalar(out=rstd[:], in0=mv[:, 1:2], scalar1=EPS, scalar2=None, op0=ALU.add), reads=["mv" + sk], writes=["rstd" + sk])
        P.op("act", lambda e: e.activation(out=rstd[:], in_=rstd[:], func=AF.Ln), reads=["rstd" + sk], writes=["rstd" + sk])
        P.op("act", lambda e: e.activation(out=rstd[:], in_=rstd[:], func=AF.Exp, scale=-0.5), reads=["rstd" + sk], writes=["rstd" + sk])
        P.op("dve", lambda e: e.scalar_tensor_tensor(out=nmr[:], in0=mv[:, 0:1], scalar=-1.0, in1=rstd[:], op0=ALU.mult, op1=ALU.mult),
             reads=["mv" + sk, "rstd" + sk], writes=["nmr" + sk])
        P.op("act", lambda e: e.activation(out=y[:], in_=res[:], func=AF.Identity, bias=nmr[:], scale=rstd[:]),
             reads=[reskey, "rstd" + sk, "nmr" + sk], writes=[outkey])
        P.op("dve", lambda e: e.tensor_tensor(out=y[:], in0=y[:], in1=st["g"][:], op=ALU.mult), reads=[outkey, "lng"], writes=[outkey])
        P.op("pool", lambda e: e.tensor_tensor(out=y[:], in0=y[:], in1=st["b"][:], op=ALU.add), reads=[outkey, "lnb"], writes=[outkey])

    def ln_tiles(stk, lrow):
        st = {}
        st["stats"] = [k.sb(stk, "ln_stats%d" % i, [128, 24]) for i in range(2)]; st["mv"] = [k.sb(stk, "ln_mv%d" % i, [128, 2]) for i in range(2)]
        st["rstd"] = [k.sb(stk, "ln_rstd%d" % i, [128, 1]) for i in range(2)]; st["nmr"] = [k.sb(stk, "ln_nmr%d" % i, [128, 1]) for i in range(2)]
        st["g"] = k.sb(stk, "ln_g", [128, D]); st["b"] = k.sb(stk, "ln_b", [128, D])
        P.dma("sp", st["g"][:], ln_d[lrow:lrow + 1, :].partition_broadcast(128), writes=["lng"])
        P.dma("sp", st["b"][:], ln_d[lrow + 1:lrow + 2, :].partition_broadcast(128), writes=["lnb"])
        return st

    def stage_proj():
        with ExitStack() as stk:
            xTb = k.sb(stk, "xTb", [128, 16, S], BF16)
            for kc in range(16):
                for hh in range(2):
                    P.dma("pool", xTb[:, kc, hh * 2048:(hh + 1) * 2048], xT_d[kc * 128:(kc + 1) * 128, hh * 2048:(hh + 1) * 2048],
                          writes=["xTb%d" % kc])
            wt = [k.sb(stk, "wt%d" % i, [128, 16, 128], BF16) for i in range(2)]
            ob = [k.sb(stk, "ob%d" % i, [128, S], F32) for i in range(2)]
            obb = [k.sb(stk, "obb%d" % i, [128, S], BF16) for i in range(2)]
            for cc in range(48):
                w = wt[cc % 2]; wk = "wt%d" % (cc % 2)
                P.dma("pool", w[:], w_in_d[:, cc * 128:(cc + 1) * 128].rearrange("(kc p) m -> p kc m", p=128), writes=[wk])
                o = (obb if cc < 24 else ob)[cc % 2]; okey = "ob%d_%d" % (cc % 2, cc < 24)
                for tb in range(8):
                    pt, pk = PS(tb % 4)
                    for kc in range(16):
                        P.op("pe", lambda e, pt=pt, w=w, kc=kc, tb=tb: e.matmul(pt[:], w[:, kc, :], xTb[:, kc, tb * 512:(tb + 1) * 512],
                                                                              start=(kc == 0), stop=(kc == 15)),
                             reads=[wk, "xTb%d" % kc], writes=[pk])
                    eng = "act" if tb % 2 == 0 else "dve"
                    if eng == "act":
                        P.op("act", lambda e, o=o, pt=pt, tb=tb: e.copy(out=o[:, tb * 512:(tb + 1) * 512], in_=pt[:]), reads=[pk], writes=[okey + "_%d" % tb])
                    else:
                        P.op("dve", lambda e, o=o, pt=pt, tb=tb: e.tensor_copy(out=o[:, tb * 512:(tb + 1) * 512], in_=pt[:]), reads=[pk], writes=[okey + "_%d" % tb])
                dst = qk_s[cc * 128:(cc + 1) * 128, :] if cc < 24 else hy_s[(cc - 24) * 128:(cc - 23) * 128, :]
                P.dma("sp", dst, o[:], reads=[okey + "_%d" % tb for tb in range(8)], writes=["scr_proj"])
            P.barrier()

    def stage_attn():
        with ExitStack() as stk:
            qT = k.sb(stk, "qT", [128, S], BF16); kT = k.sb(stk, "kT", [128, S], BF16); vT = k.sb(stk, "vT", [128, S], BF16)
            ab = k.sb(stk, "ab", [128, 3, 256], F32)
            acc_o = k.sb(stk, "acc_o", [128, S], F32); acc_l = k.sb(stk, "acc_l", [128, S], F32)
            vt = [k.sb(stk, "vt%d" % i, [128, 128], BF16) for i in range(3)]
            pt_ = [k.sb(stk, "pt%d" % i, [128, 256], BF16) for i in range(3)]
            tmp = [k.sb(stk, "atmp%d" % i, [128, 256], F32) for i in range(2)]
            ao = k.sb(stk, "ao", [128, S], BF16)
            scale = 1.0 / math.sqrt(128.0)
            cnt = 0
            for h in range(8):
                P.dma("sp", qT[:], qk_s[h * 128:(h + 1) * 128, :], writes=["qT"])
                P.dma("act", kT[:], qk_s[1024 + h * 128:1024 + (h + 1) * 128, :], writes=["kT"])
                P.dma("sp", vT[:], qk_s[2048 + h * 128:2048 + (h + 1) * 128, :], writes=["vT"])
                P.dma("act", ab[:], abias_d[:, h * 768:(h + 1) * 768].rearrange("p (a b) -> p a b", a=3), writes=["ab"])
                P.op("dve", lambda e: e.memset(acc_o[:], 0.0), writes=["acc_o"])
                P.op("pool", lambda e: e.memset(acc_l[:], 0.0), writes=["acc_l"])
                for di, (_, d) in enumerate(DIL):
                    Lc = S // d
                    nt = Lc // 128
                    for r in range(d):
                        def cls(t, i0, i1, r=r, d=d):
                            return t[:, r + d * i0: r + d * (i1 - 1) + 1: d]

                        def block(m, cls=cls, nt=nt, Lc=Lc):
                            q0, q1 = max(0, 128 * m - 64), min(Lc, 128 * m + 64)
                            n = q1 - q0
                            po, pok = PS(4); pl, plk = PS(5)
                            terms = []
                            if m - 1 >= 0:
                                c0 = q0 - (128 * (m - 1) - 64)
                                terms.append(((m - 1) % 3, c0))
                            if m <= nt - 1:
                                c0 = q0 - (128 * m - 64)
                                terms.append((m % 3, c0))
                            for ti, (bi, c0) in enumerate(terms):
                                P.op("pe", lambda e, bi=bi, c0=c0, ti=ti, n=n: e.matmul(po[:, 0:n], vt[bi][:], pt_[bi][:, c0:c0 + n],
                                                                                   start=(ti == 0), stop=(ti == len(terms) - 1)),
                                     reads=["vt%d" % bi, "pt%d" % bi], writes=[pok])
                            for ti, (bi, c0) in enumerate(terms):
                                P.op("pe", lambda e, bi=bi, c0=c0, ti=ti, n=n: e.matmul(pl[:, 0:n], onesb[:], pt_[bi][:, c0:c0 + n],
                                                                                   start=(ti == 0), stop=(ti == len(terms) - 1)),
                                     reads=["pt%d" % bi], writes=[plk])
                            P.op("dve", lambda e, q0=q0, q1=q1, n=n: e.tensor_tensor(out=cls(acc_o, q0, q1), in0=cls(acc_o, q0, q1), in1=po[:, 0:n], op=ALU.add),
                                 reads=[pok, "acc_o"], writes=["acc_o"])
                            P.op("dve", lambda e, q0=q0, q1=q1, n=n: e.tensor_tensor(out=cls(acc_l, q0, q1), in0=cls(acc_l, q0, q1), in1=pl[:, 0:n], op=ALU.add),
                                 reads=[plk, "acc_l"], writes=["acc_l"])

                        for j in range(nt):
                            bi = j % 3
                            P.op("pe", lambda e, j=j, cls=cls: e.transpose(psb[:, 0:128], cls(vT, 128 * j, 128 * j + 128), identb[:]),
                                 reads=["vT", "identb"], writes=["psb"])
                            P.op("act", lambda e, bi=bi: e.copy(out=vt[bi][:], in_=psb[:, 0:128]), reads=["psb"], writes=["vt%d" % bi])
                            q0, q1 = max(0, 128 * j - 64), min(Lc, 128 * j + 192)
                            c0 = q0 - (128 * j - 64)
                            n = q1 - q0
                            pt, pk = PS(cnt % 4)
                            tm = tmp[cnt % 2]; tk = "atmp%d" % (cnt % 2)
                            cnt += 1
                            P.op("pe", lambda e, pt=pt, j=j, q0=q0, q1=q1, c0=c0, n=n, cls=cls: e.matmul(pt[:, c0:c0 + n], cls(kT, 128 * j, 128 * j + 128), cls(qT, q0, q1),
                                                                                             start=True, stop=True),
                                 reads=["kT", "qT"], writes=[pk])
                            P.op("dve", lambda e, pt=pt, tm=tm, c0=c0, n=n, di=di: e.scalar_tensor_tensor(out=tm[:, c0:c0 + n], in0=pt[:, c0:c0 + n], scalar=scale,
                                                                                                   in1=ab[:, di, c0:c0 + n], op0=ALU.mult, op1=ALU.add),
                                 reads=[pk, "ab"], writes=[tk])
                            P.op("act", lambda e, tm=tm, bi=bi, c0=c0, n=n: e.activation(out=pt_[bi][:, c0:c0 + n], in_=tm[:, c0:c0 + n], func=AF.Exp),
                                 reads=[tk], writes=["pt%d" % bi])
                            block(j)
                            if j == nt - 1:
                                block(nt)
                P.op("dve", lambda e: e.reciprocal(out=acc_l[:], in_=acc_l[:]), reads=["acc_l"], writes=["acc_l"])
                P.op("dve", lambda e: e.tensor_tensor(out=ao[:], in0=acc_o[:], in1=acc_l[:], op=ALU.mult), reads=["acc_o", "acc_l"], writes=["ao"])
                P.dma("sp", mixT_s[h * 128:(h + 1) * 128, :], ao[:], reads=["ao"], writes=["scr_mixA"])
            P.barrier()

    def stage_hyena():
        with ExitStack() as stk:
            PI = math.pi
            gT = k.sb(stk, "gT", [64, S], F32)
            w4 = k.sb(stk, "fw4", [64, 2048], F32)
            zstk = ExitStack()
            zf = k.sb(zstk, "zf", [33, S], F32)
            w1 = k.sb(zstk, "fw1", [33, 64], F32); w2 = k.sb(zstk, "fw2", [64, 64], F32); w3 = k.sb(zstk, "fw3", [64, 64], F32)
            fv = k.sb(zstk, "fv", [64, 4], F32); frb = k.sb(zstk, "frb", [64, 3], F32); npi = k.sb(zstk, "npi", [64, 1], F32)
            P.dma("sp", zf[:], zf_d, writes=["zf"])
            P.dma("act", w1[:], fw1_d, writes=["w1"]); P.dma("act", w2[:], fw2_d, writes=["w2"]); P.dma("act", w3[:], fw3_d, writes=["w3"])
            P.dma("sp", w4[:], fw4_d, writes=["w4"]); P.dma("sp", fv[:], fvec_d, writes=["fv"])
            P.op("dve", lambda e: e.memset(npi[:], -PI), writes=["npi"])
            for i in range(3):
                P.op("dve", lambda e, i=i: e.tensor_tensor(out=frb[:, i:i + 1], in0=fv[:, i:i + 1], in1=fv[:, 3:4], op=ALU.mult), reads=["fv"], writes=["frb%d" % i])
            u = [k.sb(zstk, "fu%d" % i, [64, 512], F32) for i in range(2)]
            ki = k.sb(zstk, "fki", [64, 512], mybir.dt.int32); kf = k.sb(zstk, "fkf", [64, 512], F32)
            for nb in range(8):
                src, srck, srcp = zf, "zf", 33
                for li, w in enumerate((w1, w2, w3)):
                    pt, pk = PS(li)
                    P.op("pe", lambda e, pt=pt, w=w, src=src, srcp=srcp, nb=nb, li=li: e.matmul(
                        pt[0:64, :], w[0:srcp, :], (src[0:srcp, nb * 512:(nb + 1) * 512] if li == 0 else src[0:64, :]), start=True, stop=True),
                         reads=["w%d" % (li + 1), srck], writes=[pk])
                    uu = u[li % 2]; uk = "fu%d" % (li % 2)
                    P.op("dve", lambda e, pt=pt, uu=uu, li=li: e.tensor_scalar(out=uu[:], in0=pt[0:64, :], scalar1=fv[:, 3:4], scalar2=frb[:, li:li + 1],
                                                                           op0=ALU.mult, op1=ALU.add), reads=[pk, "fv", "frb%d" % li], writes=[uk])
                    P.op("dve", lambda e, uu=uu: e.tensor_scalar(out=ki[:], in0=uu[:], scalar1=1.0 / (2.0 * PI), scalar2=8.0, op0=ALU.mult, op1=ALU.add),
                         reads=[uk], writes=["ki"])
                    P.op("dve", lambda e: e.tensor_copy(out=kf[:], in_=ki[:]), reads=["ki"], writes=["kf"])
                    P.op("dve", lambda e, uu=uu: e.scalar_tensor_tensor(out=uu[:], in0=kf[:], scalar=-2.0 * PI, in1=uu[:], op0=ALU.mult, op1=ALU.add),
                         reads=[uk, "kf"], writes=[uk])
                    P.op("dve", lambda e, uu=uu: e.tensor_scalar(out=uu[:], in0=uu[:], scalar1=16.0 * PI, scalar2=None, op0=ALU.add), reads=[uk], writes=[uk])
                    P.op("dve", lambda e, uu=uu: e.tensor_scalar(out=kf[:], in0=uu[:], scalar1=PI, scalar2=-2.0 * PI, op0=ALU.is_gt, op1=ALU.mult),
                         reads=[uk, "kf"], writes=["kf"])
                    P.op("dve", lambda e, uu=uu: e.tensor_tensor(out=uu[:], in0=uu[:], in1=kf[:], op=ALU.add), reads=[uk, "kf"], writes=[uk])
                    P.op("dve", lambda e, uu=uu: e.tensor_scalar(out=kf[:], in0=uu[:], scalar1=-PI, scalar2=2.0 * PI, op0=ALU.is_lt, op1=ALU.mult),
                         reads=[uk, "kf"], writes=["kf"])
                    P.op("dve", lambda e, uu=uu: e.tensor_tensor(out=uu[:], in0=uu[:], in1=kf[:], op=ALU.add), reads=[uk, "kf"], writes=[uk])
                    dst = gT[:, nb * 512:(nb + 1) * 512] if li == 2 else uu[:]
                    dk = "gT" if li == 2 else uk
                    P.op("act", lambda e, uu=uu, dst=dst: e.activation(out=dst, in_=uu[:], func=AF.Sin), reads=[uk], writes=[dk])
                    src, srck, srcp = uu, uk, 64
            P.barrier()
            zstk.close()
            with ExitStack() as fs:
                ndr = k.sb(fs, "ndr", [128, HW], F32)
                tcol = k.sb(fs, "tcol", [128, NT], F32); m0 = k.sb(fs, "m0", [128, NT], F32)
                P.dma("sp", ndr[:], ndrow_d.partition_broadcast(128), writes=["ndr"])
                P.dma("sp", tcol[:], tcol_d, writes=["tcol"]); P.dma("sp", m0[:], m0col_d, writes=["m0"])
                dec = [k.sb(fs, "fdec%d" % i, [128, HW], F32) for i in range(2)]
                hfb = [k.sb(fs, "fhf%d" % i, [128, HW], F32) for i in range(2)]
                hbb = [k.sb(fs, "fhb%d" % i, [128, HW], F32) for i in range(2)]
                hso = [k.sb(fs, "fhs%d" % i, [128, HW], BF16) for i in range(2)]
                hdo = [k.sb(fs, "fhd%d" % i, [128, HW], BF16) for i in range(2)]
                for lt in range(NT):
                    b = lt % 2
                    P.op("act", lambda e, b=b, lt=lt: e.activation(out=dec[b][:], in_=ndr[:], func=AF.Exp, scale=tcol[:, lt:lt + 1]),
                         reads=["ndr", "tcol"], writes=["fdec%d" % b])
                    for nbk in range(4):
                        pt, pk = PS(nbk)
                        P.op("pe", lambda e, pt=pt, nbk=nbk, lt=lt: e.matmul(pt[:], gT[:, lt * 128:(lt + 1) * 128], w4[:, nbk * 512:(nbk + 1) * 512], start=True, stop=True),
                             reads=["gT", "w4"], writes=[pk])
                        cs = slice((nbk % 2) * 512, (nbk % 2 + 1) * 512)
                        if nbk < 2:
                            P.op("dve", lambda e, pt=pt, b=b, cs=cs: e.tensor_tensor(out=hfb[b][:, cs], in0=pt[:], in1=dec[b][:, cs], op=ALU.mult),
                                 reads=[pk, "fdec%d" % b], writes=["fhf%d_%d" % (b, nbk % 2)])
                        else:
                            P.op("dve", lambda e, pt=pt, b=b, cs=cs, lt=lt: e.scalar_tensor_tensor(out=hbb[b][:, cs], in0=pt[:], scalar=m0[:, lt:lt + 1], in1=dec[b][:, cs],
                                                                                            op0=ALU.mult, op1=ALU.mult),
                                 reads=[pk, "fdec%d" % b, "m0"], writes=["fhb%d_%d" % (b, nbk % 2)])
                    P.op("dve", lambda e, b=b: e.tensor_tensor(out=hso[b][:], in0=hfb[b][:], in1=hbb[b][:], op=ALU.add),
                         reads=["fhf%d_0" % b, "fhf%d_1" % b, "fhb%d_0" % b, "fhb%d_1" % b], writes=["fhs%d" % b])
                    P.op("pool", lambda e, b=b: e.tensor_tensor(out=hdo[b][:], in0=hbb[b][:], in1=hfb[b][:], op=ALU.subtract),
                         reads=["fhf%d_0" % b, "fhf%d_1" % b, "fhb%d_0" % b, "fhb%d_1" % b], writes=["fhd%d" % b])
                    P.dma("sp", hs_s[lt * 128:(lt + 1) * 128, :], hso[b][:], reads=["fhs%d" % b], writes=["scr_hs"])
                    P.dma("act", hd_s[lt * 128:(lt + 1) * 128, :], hdo[b][:], reads=["fhd%d" % b], writes=["scr_hd"])
                P.barrier()
            with ExitStack() as cs_:
                uu3 = [k.sb(cs_, "hu%d" % i, [128, S], F32) for i in range(3)]
                cv = [k.sb(cs_, "hc%d" % i, [128, S], F32) for i in range(3)]
                cw = k.sb(cs_, "cw", [128, 3, 3], F32); cb = k.sb(cs_, "cb", [128, 3], F32)
                zt = [k.sb(cs_, "zt%d" % i, [128, NT, 128], BF16) for i in range(2)]
                for ct in range(8):
                    for i in range(3):
                        rows = slice(i * HW + ct * 128, i * HW + (ct + 1) * 128)
                        P.dma(k.q(), uu3[i][:], hy_s[rows, :], writes=["hu%d" % i])
                        P.dma(k.q(), cw[:, i, :], convw_d[rows, :], writes=["cw%d" % i])
                        P.dma(k.q(), cb[:, i:i + 1], convb_d[rows, :], writes=["cb%d" % i])
                    for i in range(3):
                        src, dst = uu3[i], cv[i]
                        eng = "dve"
                        P.op(eng, lambda e, src=src, dst=dst, i=i: e.tensor_scalar(out=dst[:], in0=src[:], scalar1=cw[:, i, 1:2], scalar2=cb[:, i:i + 1], op0=ALU.mult, op1=ALU.add),
                             reads=["hu%d" % i, "cw%d" % i, "cb%d" % i], writes=["hc%d" % i])
                        P.op(eng, lambda e, src=src, dst=dst, i=i: e.scalar_tensor_tensor(out=dst[:, 1:S], in0=src[:, 0:S - 1], scalar=cw[:, i, 0:1], in1=dst[:, 1:S], op0=ALU.mult, op1=ALU.add),
                             reads=["hu%d" % i, "cw%d" % i, "hc%d" % i], writes=["hc%d" % i])
                        P.op(eng, lambda e, src=src, dst=dst, i=i: e.scalar_tensor_tensor(out=dst[:, 0:S - 1], in0=src[:, 1:S], scalar=cw[:, i, 2:3], in1=dst[:, 0:S - 1], op0=ALU.mult, op1=ALU.add),
                             reads=["hu%d" % i, "cw%d" % i, "hc%d" % i], writes=["hc%d" % i])
                    z = uu3[0]
                    P.op("pool", lambda e: e.tensor_tensor(out=z[:], in0=cv[2][:], in1=cv[1][:], op=ALU.mult), reads=["hc1", "hc2", "hu0"], writes=["hu0"])
                    P.dma("sp", x0_s[ct * 128:(ct + 1) * 128, :], cv[0][:], reads=["hc0"], writes=["scr_x0"])
                    P.dma("act", z_s[ct * 128:(ct + 1) * 128, :], z[:], reads=["hu0"], writes=["scr_z"])
                    ztb = zt[ct % 2]; zk = "zt%d" % (ct % 2)
                    for tt in range(NT):
                        pt, pk = PS(tt % 4)
                        P.op("pe", lambda e, pt=pt, tt=tt: e.transpose(pt[:, 0:128], z[:, tt * 128:(tt + 1) * 128], ident[:]), reads=["hu0", "ident"], writes=[pk])
                        if tt % 2 == 0:
                            P.op("act", lambda e, pt=pt, tt=tt, ztb=ztb: e.copy(out=ztb[:, tt, :], in_=pt[:, 0:128]), reads=[pk], writes=[zk + "_%d" % tt])
                        else:
                            P.op("dve", lambda e, pt=pt, tt=tt, ztb=ztb: e.tensor_copy(out=ztb[:, tt, :], in_=pt[:, 0:128]), reads=[pk], writes=[zk + "_%d" % tt])
                    for q4 in range(4):
                        P.dma("sp", ztok_s[q4 * 1024:(q4 + 1) * 1024, ct * 128:(ct + 1) * 128].rearrange("(tt p) c -> p tt c", p=128), ztb[:, q4 * 8:(q4 + 1) * 8, :],
                              reads=[zk + "_%d" % tt for tt in range(q4 * 8, q4 * 8 + 8)], writes=["scr_ztok"])
                P.barrier()
            NF = 33
            with ExitStack() as ds:
                coef = k.sb(ds, "coef", [128, NF], F32)
                P.dma("sp", coef[:], coef_d, writes=["coef"])
                Aa = k.sb(ds, "dA", [128, NF, 256], BF16); Bb = k.sb(ds, "dB", [128, NF, 256], BF16)
                for cq in range(NQ_DBG):
                    ccols = slice(cq * 256, (cq + 1) * 256)
                    with ExitStack() as f1:
                        zc = k.sb(f1, "zc", [128, NT, 512], BF16); zs = k.sb(f1, "zs", [128, NT, 512], BF16)
                        for q4 in range(4):
                            rows = slice(q4 * 1024, (q4 + 1) * 1024)
                            tts = slice(q4 * 8, (q4 + 1) * 8)
                            P.dma("sp", zc[:, tts, 0:256], ztok_s[rows, ccols].rearrange("(tt p) c -> p tt c", p=128), writes=["zc"])
                            P.dma("act", zs[:, tts, 0:256], ztok_s[rows, ccols].rearrange("(tt p) c -> p tt c", p=128), writes=["zs"])
                            P.dma("sp", zc[:, tts, 256:512], hs_s[rows, ccols].rearrange("(tt p) c -> p tt c", p=128), writes=["zc"])
                            P.dma("act", zs[:, tts, 256:512], hd_s[rows, ccols].rearrange("(tt p) c -> p tt c", p=128), writes=["zs"])
                        tc_ = [k.sb(f1, "tc%d" % i, [128, NT, 128], BF16) for i in range(2)]
                        ts_ = [k.sb(f1, "ts%d" % i, [128, NT, 128], BF16) for i in range(2)]
                        kr = k.sb(f1, "kr", [128, 256], F32); ki_ = k.sb(f1, "kiq", [128, 256], F32)
                        t1 = k.sb(f1, "dt1", [128, 256], F32); t2 = k.sb(f1, "dt2", [128, 256], F32)
                        for ft in range(NF):
                            b = ft % 2
                            P.dma("sp", tc_[b][:], dftcF_d[ft].rearrange("p (tt f) -> p tt f", f=128), writes=["tc%d" % b])
                            P.dma("act", ts_[b][:], dftsF_d[ft].rearrange("p (tt f) -> p tt f", f=128), writes=["ts%d" % b])
                            pc, pck = PS(b); psn, psk = PS(2 + b)
                            for tt in range(NT):
                                P.op("pe", lambda e, pc=pc, b=b, tt=tt: e.matmul(pc[:], tc_[b][:, tt, :], zc[:, tt, :], start=(tt == 0), stop=(tt == NT - 1)),
                                     reads=["tc%d" % b, "zc"], writes=[pck])
                            for tt in range(NT):
                                P.op("pe", lambda e, psn=psn, b=b, tt=tt: e.matmul(psn[:], ts_[b][:, tt, :], zs[:, tt, :], start=(tt == 0), stop=(tt == NT - 1)),
                                     reads=["ts%d" % b, "zs"], writes=[psk])
                            P.op("act", lambda e, pc=pc, ft=ft: e.activation(out=kr[:], in_=pc[:, 256:512], func=AF.Identity, scale=coef[:, ft:ft + 1]), reads=[pck, "coef"], writes=["kr"])
                            P.op("act", lambda e, psn=psn, ft=ft: e.activation(out=ki_[:], in_=psn[:, 256:512], func=AF.Identity, scale=coef[:, ft:ft + 1]), reads=[psk, "coef"], writes=["kiq"])
                            P.op("dve", lambda e, pc=pc: e.tensor_tensor(out=t1[:], in0=pc[:, 0:256], in1=kr[:], op=ALU.mult), reads=[pck, "kr"], writes=["dt1"])
                            P.op("dve", lambda e, psn=psn: e.tensor_tensor(out=t2[:], in0=psn[:, 0:256], in1=ki_[:], op=ALU.mult), reads=[psk, "kiq"], writes=["dt2"])
                            P.op("dve", lambda e, ft=ft: e.tensor_tensor(out=Aa[:, ft, :], in0=t1[:], in1=t2[:], op=ALU.add), reads=["dt1", "dt2"], writes=["dA%d" % ft])
                            P.op("dve", lambda e, psn=psn: e.tensor_tensor(out=t1[:], in0=psn[:, 0:256], in1=kr[:], op=ALU.mult), reads=[psk, "kr", "dt1"], writes=["dt1"])
                            P.op("dve", lambda e, pc=pc: e.tensor_tensor(out=t2[:], in0=pc[:, 0:256], in1=ki_[:], op=ALU.mult), reads=[pck, "kiq", "dt2"], writes=["dt2"])
                            P.op("dve", lambda e, ft=ft: e.tensor_tensor(out=Bb[:, ft, :], in0=t1[:], in1=t2[:], op=ALU.subtract), reads=["dt1", "dt2"], writes=["dB%d" % ft])
                        P.barrier()
                    with ExitStack() as f2:
                        ic = [k.sb(f2, "ic%d" % i, [128, NF, 256], BF16) for i in range(2)]
                        isn = [k.sb(f2, "is%d" % i, [128, NF, 256], BF16) for i in range(2)]
                        x0t = [k.sb(f2, "x0t%d" % i, [128, 256], F32) for i in range(2)]
                        zt2 = [k.sb(f2, "zt2%d" % i, [128, 256], F32) for i in range(2)]
                        ot = [k.sb(f2, "hot%d" % i, [128, 256], BF16) for i in range(2)]
                        hbias = k.sb(f2, "hbias", [128, 2], F32)
                        for c2 in range(2):
                            P.dma("sp", hbias[:, c2:c2 + 1], hyb_d[cq * 256 + c2 * 128: cq * 256 + (c2 + 1) * 128, :], writes=["hbias"])
                        n = 0
                        for tb in range(16):
                            b = tb % 2
                            tcols = slice(tb * 256, (tb + 1) * 256)
                            P.dma("sp", ic[b][:], dftcI_d[tb].rearrange("p (ft t) -> p ft t", t=256), writes=["ic%d" % b])
                            P.dma("act", isn[b][:], dftsI_d[tb].rearrange("p (ft t) -> p ft t", t=256), writes=["is%d" % b])
                            for c2 in range(2):
                                pt, pk = PS(n % 4)
                                nb2 = n % 2; n += 1
                                crow = slice(cq * 256 + c2 * 128, cq * 256 + (c2 + 1) * 128)
                                P.dma("sp", x0t[nb2][:], x0_s[crow, tcols], writes=["x0t%d" % nb2])
                                P.dma("act", zt2[nb2][:], z_s[crow, tcols], writes=["zt2%d" % nb2])
                                for ft in range(NF):
                                    P.op("pe", lambda e, pt=pt, ft=ft, c2=c2, b=b: e.matmul(pt[:, 0:256], Aa[:, ft, c2 * 128:(c2 + 1) * 128], ic[b][:, ft, :], start=(ft == 0), stop=False),
                                         reads=["dA%d" % ft, "ic%d" % b], writes=[pk])
                                for ft in range(NF):
                                    P.op("pe", lambda e, pt=pt, ft=ft, c2=c2, b=b: e.matmul(pt[:, 0:256], Bb[:, ft, c2 * 128:(c2 + 1) * 128], isn[b][:, ft, :], start=False, stop=(ft == NF - 1)),
                                         reads=["dB%d" % ft, "is%d" % b], writes=[pk])
                                P.op("dve", lambda e, pt=pt, nb2=nb2, c2=c2: e.scalar_tensor_tensor(out=zt2[nb2][:], in0=zt2[nb2][:], scalar=hbias[:, c2:c2 + 1], in1=pt[:, 0:256],
                                                                                                 op0=ALU.mult, op1=ALU.add), reads=[pk, "zt2%d" % nb2, "hbias"], writes=["zt2%d" % nb2])
                                P.op("dve", lambda e, nb2=nb2: e.tensor_tensor(out=ot[nb2][:], in0=zt2[nb2][:], in1=x0t[nb2][:], op=ALU.mult),
                                     reads=["zt2%d" % nb2, "x0t%d" % nb2], writes=["hot%d" % nb2])
                                P.dma("sp", mixT_s[HW + cq * 256 + c2 * 128: HW + cq * 256 + (c2 + 1) * 128, tcols], ot[nb2][:], reads=["hot%d" % nb2], writes=["scr_mixH"])
                        P.barrier()
            P.barrier()

    def router_tail(stk, rt, y, ykey, tt):
        yT = rt["yT"]
        for kc in range(16):
            pt, pk = PS(4 + kc % 2)
            P.op("pe", lambda e, pt=pt, kc=kc: e.transpose(pt[:, 0:128], y[:, kc * 128:(kc + 1) * 128], ident[:]), reads=[ykey, "ident"], writes=[pk])
            eng = "act" if kc % 2 == 0 else "dve"
            if eng == "act":
                P.op("act", lambda e, pt=pt, kc=kc: e.copy(out=yT[:, kc, :], in_=pt[:, 0:128]), reads=[pk], writes=["yT%d" % kc])
            else:
                P.op("dve", lambda e, pt=pt, kc=kc: e.tensor_copy(out=yT[:, kc, :], in_=pt[:, 0:128]), reads=[pk], writes=["yT%d" % kc])
        pt, pk = PS(6)
        for kc in range(16):
            P.op("pe", lambda e, kc=kc: e.matmul(pt[:, 0:E], yT[:, kc, :], rt["rw"][:, kc, :], start=(kc == 0), stop=(kc == 15)),
                 reads=["yT%d" % kc, "rw"], writes=[pk])
        mx, sm, ex = rt["mx"], rt["sm"], rt["ex"]
        P.op("dve", lambda e: e.reduce_max(out=mx[:], in_=pt[:, 0:E], axis=AX.X), reads=[pk], writes=["mx"])
        P.op("dve", lambda e: e.tensor_scalar(out=ex[:], in0=pt[:, 0:E], scalar1=mx[:], scalar2=None, op0=ALU.subtract), reads=[pk, "mx"], writes=["ex"])
        P.op("act", lambda e: e.activation(out=ex[:], in_=ex[:], func=AF.Exp), reads=["ex"], writes=["ex"])
        P.op("dve", lambda e: e.reduce_sum(out=sm[:], in_=ex[:], axis=AX.X), reads=["ex"], writes=["sm"])
        P.op("dve", lambda e: e.reciprocal(out=sm[:], in_=sm[:]), reads=["sm"], writes=["sm"])
        P.op("dve", lambda e: e.tensor_scalar(out=rt["aff_tok"][:, tt, :], in0=ex[:], scalar1=sm[:], scalar2=None, op0=ALU.mult), reads=["ex", "sm"], writes=["aff_tok"])
        pt2, pk2 = PS(5)
        P.op("pe", lambda e: e.transpose(pt2[0:E, 0:128], rt["aff_tok"][:, tt, :], ident[:]), reads=["aff_tok", "ident"], writes=[pk2])
        P.op("dve", lambda e: e.tensor_copy(out=rt["affT"][:, tt * 128:(tt + 1) * 128], in_=pt2[0:E, 0:128]), reads=[pk2], writes=["affT"])

    def router_tiles(stk, layer, gl):
        rt = {}
        rt["yT"] = k.sb(stk, "yT", [128, 16, 128], F32)
        rt["rw"] = k.sb(stk, "rw", [128, 16, E], F32)
        P.dma("sp", rt["rw"][:], rw_d[layer].rearrange("(kc p) e -> p kc e", p=128), writes=["rw"])
        rt["mx"] = k.sb(stk, "mx", [128, 1]); rt["sm"] = k.sb(stk, "sm", [128, 1]); rt["ex"] = k.sb(stk, "ex", [128, E])
        rt["aff_tok"] = gl["aff_tok"]; rt["affT"] = k.sb(stk, "affT", [E, S], F32)
        return rt

    def stage_mix_out(layer, gl, mixer):
        with ExitStack() as stk:
            lt = ln_tiles(stk, layer * 4)
            rt = router_tiles(stk, layer, gl)
            xr = [k.sb(stk, "xr%d" % i, [128, D], F32) for i in range(2)]
            res2 = [k.sb(stk, "res%d" % i, [128, D], F32) for i in range(2)]
            y = [k.sb(stk, "yln%d" % i, [128, D], F32) for i in range(2)]
            yb = [k.sb(stk, "ylnb%d" % i, [128, D], BF16) for i in range(2)]
            mx_fn = mixer(stk)
            xin = x_d if layer == 0 else x2_s
            for tt in range(NT):
                b = tt % 2
                P.dma("act", xr[b][:], xin[tt * 128:(tt + 1) * 128, :], writes=["xr%d" % b])
                res = res2[b]
                mx_fn(tt, xr[b], "xr%d" % b, res, "res%d" % b)
                layer_norm(lt, res, "res%d" % b, b, "yln%d" % b, y[b])
                P.dma("sp", x1_s[tt * 128:(tt + 1) * 128, :], y[b][:], reads=["yln%d" % b], writes=["scr_x1"])
                P.op("pool", lambda e, b=b: e.tensor_copy(out=yb[b][:], in_=y[b][:]), reads=["yln%d" % b], writes=["ylnb%d" % b])
                P.dma("act", x1b_s[tt * 128:(tt + 1) * 128, :], yb[b][:], reads=["ylnb%d" % b], writes=["scr_x1b"])
                router_tail(stk, rt, y[b], "yln%d" % b, tt)
            P.dma("act", affT_s, rt["affT"][:], reads=["affT"], writes=["scr_affT"])
            P.barrier()

    def mixer_attnhy(stk):
        wo = k.sb(stk, "wo", [128, 16, D], BF16)
        for kc in range(16):
            P.dma("pool", wo[:, kc, :], w_out_d[kc * 128:(kc + 1) * 128, :], writes=["wo"])
        mt = [k.sb(stk, "mt%d" % i, [128, 16, 128], BF16) for i in range(2)]

        def fn(tt, xr, xrk, res, resk):
            m = mt[tt % 2]; mk = "mt%d" % (tt % 2)
            P.dma("sp", m[:], mixT_s[:, tt * 128:(tt + 1) * 128].rearrange("(kc p) t -> p kc t", p=128), writes=[mk])
            for nb in range(4):
                pt, pk = PS(nb)
                for kc in range(16):
                    P.op("pe", lambda e, pt=pt, kc=kc, nb=nb, m=m: e.matmul(pt[:], m[:, kc, :], wo[:, kc, nb * 512:(nb + 1) * 512], start=(kc == 0), stop=(kc == 15)),
                         reads=[mk, "wo"], writes=[pk])
                P.op("dve", lambda e, pt=pt, nb=nb, xr=xr: e.scalar_tensor_tensor(out=res[:, nb * 512:(nb + 1) * 512], in0=xr[:, nb * 512:(nb + 1) * 512], scalar=ALPHA,
                                                                                in1=pt[:], op0=ALU.mult, op1=ALU.add), reads=[pk, xrk], writes=[resk])
        return fn

    def mixer_pool(stk):
        pl = [k.sb(stk, "plr%d" % i, [128, D], F32) for i in range(2)]

        def fn(tt, xr, xrk, res, resk):
            p = pl[tt % 2]; pk = "plr%d" % (tt % 2)
            P.dma("sp", p[:], pl_s[tt * 128:(tt + 1) * 128, :], writes=[pk])
            P.op("dve", lambda e: e.scalar_tensor_tensor(out=res[:], in0=xr[:], scalar=ALPHA, in1=p[:], op0=ALU.mult, op1=ALU.add), reads=[pk, xrk], writes=[resk])
        return fn

    def stage_select(gl):
        with ExitStack() as stk:
            affT = k.sb(stk, "affT", [E, S], F32)
            P.dma("sp", affT[:], affT_s, writes=["affT"])
            lo = k.sb(stk, "lo", [E, 1]); hi = k.sb(stk, "hi", [E, 1]); mid = k.sb(stk, "mid", [E, 1]); cnt = k.sb(stk, "cnt", [E, 1])
            ge = k.sb(stk, "ge", [E, 1]); t1 = k.sb(stk, "t1", [E, 1])
            m = k.sb(stk, "msk", [E, S]); c2 = k.sb(stk, "csum", [E, S]); ones = k.sb(stk, "ones_s", [E, S])
            P.op("dve", lambda e: e.memset(lo[:], 0.0), writes=["lo"])
            P.op("dve", lambda e: e.memset(hi[:], 1.0), writes=["hi"])
            P.op("dve", lambda e: e.memset(ones[:], 1.0), writes=["ones_s"])
            for it in range(34):
                P.op("dve", lambda e: e.tensor_tensor(out=mid[:], in0=lo[:], in1=hi[:], op=ALU.add), reads=["lo", "hi"], writes=["mid"])
                P.op("dve", lambda e: e.tensor_scalar(out=mid[:], in0=mid[:], scalar1=0.5, scalar2=None, op0=ALU.mult), reads=["mid"], writes=["mid"])
                P.op("dve", lambda e: e.tensor_scalar(out=m[:], in0=affT[:], scalar1=mid[:], scalar2=None, op0=ALU.is_ge), reads=["affT", "mid"], writes=["msk"])
                P.op("dve", lambda e: e.reduce_sum(out=cnt[:], in_=m[:], axis=AX.X), reads=["msk"], writes=["cnt"])
                P.op("dve", lambda e: e.tensor_scalar(out=ge[:], in0=cnt[:], scalar1=float(CAP) - 0.5, scalar2=None, op0=ALU.is_ge), reads=["cnt"], writes=["ge"])
                P.op("dve", lambda e: e.tensor_tensor(out=t1[:], in0=mid[:], in1=lo[:], op=ALU.subtract), reads=["mid", "lo"], writes=["t1"])
                P.op("dve", lambda e: e.scalar_tensor_tensor(out=lo[:], in0=t1[:], scalar=ge[:], in1=lo[:], op0=ALU.mult, op1=ALU.add), reads=["t1", "ge", "lo"], writes=["lo"])
                P.op("dve", lambda e: e.tensor_tensor(out=t1[:], in0=hi[:], in1=mid[:], op=ALU.subtract), reads=["mid", "hi"], writes=["t1"])
                P.op("dve", lambda e: e.scalar_tensor_tensor(out=hi[:], in0=t1[:], scalar=ge[:], in1=mid[:], op0=ALU.mult, op1=ALU.add), reads=["t1", "ge", "mid"], writes=["hi"])
            P.op("dve", lambda e: e.tensor_scalar(out=m[:], in0=affT[:], scalar1=lo[:], scalar2=None, op0=ALU.is_ge), reads=["affT", "lo"], writes=["msk"])
            P.op("dve", lambda e: e.tensor_tensor_scan(out=c2[:], data0=ones[:], data1=m[:], initial=0.0, op0=ALU.mult, op1=ALU.add), reads=["msk", "ones_s"], writes=["csum"])
            P.op("dve", lambda e: e.tensor_tensor(out=c2[:], in0=c2[:], in1=m[:], op=ALU.mult), reads=["msk", "csum"], writes=["csum"])
            P.op("dve", lambda e: e.tensor_scalar(out=c2[:], in0=c2[:], scalar1=-1.0, scalar2=None, op0=ALU.add), reads=["csum"], writes=["csum"])
            P.dma("sp", slot_s, c2[:], reads=["csum"], writes=["scr_slot"])
            slotT = gl["slotT"]
            for tt in range(NT):
                pt, pk = PS(tt % 4)
                P.op("pe", lambda e, pt=pt, tt=tt: e.transpose(pt[:, 0:E], c2[:, tt * 128:(tt + 1) * 128], ident[0:E, 0:E]), reads=["csum", "ident"], writes=[pk])
                P.op("dve", lambda e, pt=pt, tt=tt: e.tensor_copy(out=slotT[:, tt, :], in_=pt[:, 0:E]), reads=[pk], writes=["slotT"])
            P.barrier()

    def stage_experts(layer, gl):
        U32 = mybir.dt.uint32
        with ExitStack() as stk:
            slotT = gl["slotT"]; aff_tok = gl["aff_tok"]
            iota = k.sb(stk, "iota", [128, 512], F32)
            P.dma("sp", iota[:], iota_d.partition_broadcast(128), writes=["iota"])
            aux2 = [k.sb(stk, "aux%d" % i, [128, NT, 4], BF16) for i in range(2)]
            for i in range(2):
                P.dma("pool", aux2[i][:, :, 0:2], auxc_d, writes=["aux01"])
            hif2 = [k.sb(stk, "hif%d" % i, [128, NT], F32) for i in range(2)]
            pis2 = [k.sb(stk, "pis%d" % i, [128, 16], F32) for i in range(2)]
            Pm = k.sb(stk, "Pm", [128, NT, 512], BF16)
            xs = [k.sb(stk, "xs%d" % i, [128, D], BF16) for i in range(4)]
            xsT = k.sb(stk, "xsT", [128, 16, 512], BF16)
            hT = k.sb(stk, "hT", [128, 16, 512], BF16)
            idxf2 = [k.sb(stk, "idxf%d" % i, [128, 4], F32) for i in range(2)]; idxi2 = [k.sb(stk, "idxi%d" % i, [128, 4], U32) for i in range(2)]
            gate2 = [k.sb(stk, "gate%d" % i, [128, 4], F32) for i in range(2)]
            w1t = [k.sb(stk, "w1t%d" % i, [128, 16, 256], BF16) for i in range(2)]
            w3t = [k.sb(stk, "w3t%d" % i, [128, 16, 256], BF16) for i in range(2)]
            w2b = k.sb(stk, "w2b", [128, 16, D], BF16)
            sl = [k.sb(stk, "sl%d" % i, [128, 512], F32) for i in range(2)]
            yo = [k.sb(stk, "yo%d" % i, [128, D], F32) for i in range(2)]
            P.op("dve", lambda e: e.memset(yo[0][:], 0.0), writes=["yo0_%d" % nb for nb in range(4)])
            for tt in range(NT):
                P.dma(k.q(), moe_s[tt * 128:(tt + 1) * 128, :], yo[0][:], reads=["yo0_%d" % nb for nb in range(4)], writes=["moe_acc"])
            def pro_a(ex):
                par = ex % 2
                for tt in range(NT):
                    eng = "dve" if tt % 2 == 0 else "pool"
                    P.op(eng, lambda e, tt=tt, ex=ex: e.tensor_scalar(out=Pm[:, tt, :], in0=iota[:], scalar1=slotT[:, tt, ex:ex + 1], scalar2=None, op0=ALU.is_equal),
                         reads=["iota", "slotT"], writes=["Pm%d" % tt])
                ax = aux2[par]
                P.op("dve", lambda e, ex=ex, ax=ax: e.tensor_copy(out=ax[:, :, 2], in_=aff_tok[:, :, ex]), reads=["aff_tok"], writes=["aux2_%d" % par])
                P.op("dve", lambda e, ax=ax, par=par: e.tensor_copy(out=hif2[par][:], in_=ax[:, :, 2]), reads=["aux2_%d" % par], writes=["hif%d" % par])
                P.op("dve", lambda e, ex=ex, ax=ax, par=par: e.tensor_tensor(out=ax[:, :, 3], in0=aff_tok[:, :, ex], in1=hif2[par][:], op=ALU.subtract),
                     reads=["aff_tok", "hif%d" % par], writes=["aux3_%d" % par])

            def pro_b(ex):
                par = ex % 2
                ax = aux2[par]
                pi, pik = PS(6)
                for sc in range(4):
                    for tt in range(NT):
                        P.op("pe", lambda e, sc=sc, tt=tt, ax=ax: e.matmul(pi[:, sc * 4:(sc + 1) * 4], Pm[:, tt, sc * 128:(sc + 1) * 128], ax[:, tt, :], start=(tt == 0), stop=(tt == NT - 1)),
                             reads=["Pm%d" % tt, "aux01", "aux2_%d" % par, "aux3_%d" % par], writes=[pik])
                pis = pis2[par]; idxf = idxf2[par]; idxi = idxi2[par]; gate = gate2[par]
                P.op("dve", lambda e, pis=pis: e.tensor_copy(out=pis[:], in_=pi[:, 0:16]), reads=[pik], writes=["pis%d" % par])
                for sc in range(4):
                    P.op("dve", lambda e, sc=sc, pis=pis, idxf=idxf: e.scalar_tensor_tensor(out=idxf[:, sc:sc + 1], in0=pis[:, sc * 4 + 1:sc * 4 + 2], scalar=128.0, in1=pis[:, sc * 4:sc * 4 + 1],
                                                                                       op0=ALU.mult, op1=ALU.add), reads=["pis%d" % par], writes=["idxf%d" % par])
                    P.op("dve", lambda e, sc=sc, pis=pis, gate=gate: e.tensor_tensor(out=gate[:, sc:sc + 1], in0=pis[:, sc * 4 + 2:sc * 4 + 3], in1=pis[:, sc * 4 + 3:sc * 4 + 4], op=ALU.add),
                         reads=["pis%d" % par], writes=["gate%d" % par])
                P.op("dve", lambda e, idxf=idxf, idxi=idxi: e.tensor_copy(out=idxi[:], in_=idxf[:]), reads=["idxf%d" % par], writes=["idxi%d" % par])
                for sc in range(4):
                    xx = xs[sc]; xk = "xs%d" % sc
                    P.idma(lambda e, xx=xx, sc=sc, idxi=idxi: e.indirect_dma_start(out=xx[:], out_offset=None, in_=x1b_s,
                                                                                  in_offset=bass.IndirectOffsetOnAxis(ap=idxi[:, sc:sc + 1], axis=0)),
                           reads=["idxi%d" % par], writes=[xk])

            def main1(ex):
                for kc in range(16):
                    P.dma("pool", w2b[:, kc, :], ew2_d[layer, ex, kc * 128:(kc + 1) * 128, :], writes=["w2b"])
                for sc in range(4):
                    xx = xs[sc]; xk = "xs%d" % sc
                    for half in range(2):
                        for j in range(8):
                            kc = half * 8 + j
                            P.op("pe", lambda e, xx=xx, kc=kc, j=j: e.transpose(psb[:, j * 128:(j + 1) * 128], xx[:, kc * 128:(kc + 1) * 128], identb[:]),
                                 reads=[xk, "identb"], writes=["psb"])
                        dst = xsT[:, half * 8:(half + 1) * 8, sc * 128:(sc + 1) * 128]
                        src = psb[:, :].rearrange("p (a b) -> p a b", a=8)
                        wk = ["xsT%d" % kc for kc in range(half * 8, half * 8 + 8)]
                        if half == 0:
                            P.op("act", lambda e, dst=dst, src=src: e.copy(out=dst, in_=src), reads=["psb"], writes=wk)
                        else:
                            P.op("dve", lambda e, dst=dst, src=src: e.tensor_copy(out=dst, in_=src), reads=["psb"], writes=wk)
                for fc in range(16):
                    fq = fc // 2
                    wa = w1t[fq % 2]; wb = w3t[fq % 2]; wak = "w1t%d" % (fq % 2); wbk = "w3t%d" % (fq % 2)
                    if fc % 2 == 0 and fq >= 2:
                        P.dma("pool", wa[:], ew1_d[layer, ex, :, fq * 256:(fq + 1) * 256].rearrange("(kc p) m -> p kc m", p=128), writes=[wak])
                        P.dma("pool", wb[:], ew3_d[layer, ex, :, fq * 256:(fq + 1) * 256].rearrange("(kc p) m -> p kc m", p=128), writes=[wbk])
                    fo = (fc % 2) * 128
                    pa, pak = PS(4 + fc % 2); pb, pbk = PS(0 + fc % 2)
                    for kc in range(16):
                        P.op("pe", lambda e, pa=pa, wa=wa, kc=kc, fo=fo: e.matmul(pa[:], wa[:, kc, fo:fo + 128], xsT[:, kc, :], start=(kc == 0), stop=(kc == 15)),
                             reads=[wak, "xsT%d" % kc], writes=[pak])
                    for kc in range(16):
                        P.op("pe", lambda e, pb=pb, wb=wb, kc=kc, fo=fo: e.matmul(pb[:], wb[:, kc, fo:fo + 128], xsT[:, kc, :], start=(kc == 0), stop=(kc == 15)),
                             reads=[wbk, "xsT%d" % kc], writes=[pbk])
                    s_ = sl[fc % 2]; sk = "sl%d" % (fc % 2)
                    P.op("act", lambda e, pa=pa, s_=s_: e.activation(out=s_[:], in_=pa[:], func=AF.Silu), reads=[pak], writes=[sk])
                    P.op("dve", lambda e, pb=pb, s_=s_, fc=fc: e.tensor_tensor(out=hT[:, fc, :], in0=s_[:], in1=pb[:], op=ALU.mult), reads=[sk, pbk], writes=["hT%d" % fc])

            def down(ex, scs):
                par = ex % 2
                idxi = idxi2[par]; gate = gate2[par]
                for sc in scs:
                    yy = yo[sc % 2]; yk = "yo%d" % (sc % 2)
                    for nb in range(4):
                        pt, pk = PS(nb)
                        for fc in range(16):
                            P.op("pe", lambda e, pt=pt, fc=fc, sc=sc, nb=nb: e.matmul(pt[:], hT[:, fc, sc * 128:(sc + 1) * 128], w2b[:, fc, nb * 512:(nb + 1) * 512],
                                                                                   start=(fc == 0), stop=(fc == 15)), reads=["hT%d" % fc, "w2b"], writes=[pk])
                        if nb % 2 == 0:
                            P.op("act", lambda e, pt=pt, yy=yy, nb=nb, sc=sc, gate=gate: e.activation(out=yy[:, nb * 512:(nb + 1) * 512], in_=pt[:], func=AF.Identity, scale=gate[:, sc:sc + 1]),
                                 reads=[pk, "gate%d" % par], writes=[yk + "_%d" % nb])
                        else:
                            P.op("dve", lambda e, pt=pt, yy=yy, nb=nb, sc=sc, gate=gate: e.tensor_scalar(out=yy[:, nb * 512:(nb + 1) * 512], in0=pt[:], scalar1=gate[:, sc:sc + 1], scalar2=None, op0=ALU.mult),
                                 reads=[pk, "gate%d" % par], writes=[yk + "_%d" % nb])
                    P.idma(lambda e, yy=yy, sc=sc, idxi=idxi: e.indirect_dma_start(out=moe_s, out_offset=bass.IndirectOffsetOnAxis(ap=idxi[:, sc:sc + 1], axis=0), in_=yy[:], in_offset=None,
                                                                                  compute_op=ALU.add),
                           reads=["idxi%d" % par, "moe_acc"] + [yk + "_%d" % nb for nb in range(4)], writes=["moe_acc"])

            def pre_w(ex):
                for fq in range(2):
                    P.dma("pool", w1t[fq][:], ew1_d[layer, ex, :, fq * 256:(fq + 1) * 256].rearrange("(kc p) m -> p kc m", p=128), writes=["w1t%d" % fq])
                    P.dma("pool", w3t[fq][:], ew3_d[layer, ex, :, fq * 256:(fq + 1) * 256].rearrange("(kc p) m -> p kc m", p=128), writes=["w3t%d" % fq])

            pro_a(0)
            pro_b(0)
            pre_w(0)
            for ex in range(E):
                main1(ex)
                if ex + 1 < E:
                    pro_a(ex + 1)
                down(ex, (0, 1))
                if ex + 1 < E:
                    pro_b(ex + 1)
                    pre_w(ex + 1)
                down(ex, (2, 3))
            P.barrier()

    def stage_return():
        with ExitStack() as stk:
            pidx = k.sb(stk, "pidx", [128, 4], F32)
            P.dma("sp", pidx[:], pidx_d, writes=["pidx"])
            yall = k.sb(stk, "yall", [128, 64, 512], BF16)
            sbc = [k.sb(stk, "sbc%d" % i, [128, E, 128], F32) for i in range(2)]
            abc = [k.sb(stk, "abc%d" % i, [128, E, 128], F32) for i in range(2)]
            pg = [k.sb(stk, "pg%d" % i, [128, 64, 128], BF16) for i in range(2)]
            mo = [k.sb(stk, "mo%d" % i, [128, 512], F32) for i in range(2)]
            for nb in range(4):
                for ec in range(64):
                    P.dma(k.q(), yall[:, ec, :], y_s[ec * 128:(ec + 1) * 128, nb * 512:(nb + 1) * 512], writes=["yall"])
                for tt in range(NT):
                    b = tt % 2
                    P.dma("sp", sbc[b][:], slot_s[:, tt * 128:(tt + 1) * 128].partition_broadcast(128), writes=["sbc%d" % b])
                    P.dma("act", abc[b][:], affT_s[:, tt * 128:(tt + 1) * 128].partition_broadcast(128), writes=["abc%d" % b])
                    for ex in range(E):
                        for sc in range(4):
                            eng = "dve"
                            P.op(eng, lambda e, ex=ex, sc=sc, b=b: e.scalar_tensor_tensor(out=pg[b][:, ex * 4 + sc, :], in0=sbc[b][:, ex, :], scalar=pidx[:, sc:sc + 1],
                                                                                         in1=abc[b][:, ex, :], op0=ALU.is_equal, op1=ALU.mult),
                                 reads=["sbc%d" % b, "abc%d" % b, "pidx"], writes=["pg%d_%d" % (b, ex * 4 + sc)])
                    pt, pk = PS(tt % 4)
                    for ec in range(64):
                        P.op("pe", lambda e, pt=pt, ec=ec, b=b: e.matmul(pt[:], pg[b][:, ec, :], yall[:, ec, :], start=(ec == 0), stop=(ec == 63)),
                             reads=["pg%d_%d" % (b, ec), "yall"], writes=[pk])
                    P.op("act", lambda e, pt=pt, b=b: e.copy(out=mo[b][:], in_=pt[:]), reads=[pk], writes=["mo%d" % b])
                    P.dma("sp", moe_s[tt * 128:(tt + 1) * 128, nb * 512:(nb + 1) * 512], mo[b][:], reads=["mo%d" % b], writes=["scr_moe"])
            P.barrier()

    def stage_ffn_out(layer, final):
        with ExitStack() as stk:
            lt = ln_tiles(stk, layer * 4 + 2)
            xr = [k.sb(stk, "fxr%d" % i, [128, D], F32) for i in range(2)]
            mr = [k.sb(stk, "fmr%d" % i, [128, D], F32) for i in range(2)]
            fres2 = [k.sb(stk, "fres%d" % i, [128, D], F32) for i in range(2)]
            y = [k.sb(stk, "fy%d" % i, [128, D], F32) for i in range(2)]
            yT = [k.sb(stk, "fyT%d" % i, [128, 16, 128], F32) for i in range(2)]
            for tt in range(NT):
                b = tt % 2
                P.dma("act", xr[b][:], x1_s[tt * 128:(tt + 1) * 128, :], writes=["fxr%d" % b])
                P.dma("sp", mr[b][:], moe_s[tt * 128:(tt + 1) * 128, :], writes=["fmr%d" % b])
                res = fres2[b]
                P.op("dve", lambda e, b=b, res=res: e.scalar_tensor_tensor(out=res[:], in0=xr[b][:], scalar=ALPHA, in1=mr[b][:], op0=ALU.mult, op1=ALU.add),
                     reads=["fxr%d" % b, "fmr%d" % b], writes=["fres%d" % b])
                layer_norm(lt, res, "fres%d" % b, b, "fy%d" % b, y[b])
                if final:
                    P.dma("sp", out_d[tt * 128:(tt + 1) * 128, :], y[b][:], reads=["fy%d" % b], writes=["out"], is_output=True)
                else:
                    P.dma("sp", x2_s[tt * 128:(tt + 1) * 128, :], y[b][:], reads=["fy%d" % b], writes=["scr_x2"])
                    for kc in range(16):
                        pt, pk = PS(kc % 4)
                        P.op("pe", lambda e, pt=pt, kc=kc, b=b: e.transpose(pt[:, 0:128], y[b][:, kc * 128:(kc + 1) * 128], ident[:]), reads=["fy%d" % b, "ident"], writes=[pk])
                        if kc % 2 == 0:
                            P.op("act", lambda e, pt=pt, kc=kc, b=b: e.copy(out=yT[b][:, kc, :], in_=pt[:, 0:128]), reads=[pk], writes=["fyT%d_%d" % (b, kc)])
                        else:
                            P.op("dve", lambda e, pt=pt, kc=kc, b=b: e.tensor_copy(out=yT[b][:, kc, :], in_=pt[:, 0:128]), reads=[pk], writes=["fyT%d_%d" % (b, kc)])
                    P.dma("act", x2T_s[:, tt * 128:(tt + 1) * 128].rearrange("(kc p) t -> p kc t", p=128), yT[b][:],
                          reads=["fyT%d_%d" % (b, kc) for kc in range(16)], writes=["scr_x2T"])
            P.barrier()

    def stage_pool():
        with ExitStack() as stk:
            xT = [k.sb(stk, "pxT%d" % i, [128, S], F32) for i in range(4)]
            acc = k.sb(stk, "pacc", [128, S], F32)
            dT = k.sb(stk, "pdT", [128, 4, S], BF16)
            pinv = k.sb(stk, "pinv", [128, S], F32)
            pw = k.sb(stk, "ppw", [128, 4, 512], BF16)
            psc = k.sb(stk, "ppsc", [128, 512], F32)
            po = [k.sb(stk, "ppo%d" % i, [128, 512], F32) for i in range(2)]
            for gi, win in enumerate((2, 4, 8, 16)):
                hw = win // 2
                P.dma("sp", pinv[:], pinv_d[gi:gi + 1, :].partition_broadcast(128), writes=["pinv"])
                P.dma("sp", psc[:], pscale_d[0:1, gi * 512:(gi + 1) * 512].partition_broadcast(128), writes=["ppsc"])
                for kc in range(4):
                    P.dma("pool", pw[:, kc, :], poolw_d[gi, kc * 128:(kc + 1) * 128, :], writes=["ppw"])
                for ci in range(4):
                    x_ = xT[ci]; xk = "pxT%d" % ci
                    P.dma(k.q(), x_[:], x2T_s[gi * 512 + ci * 128: gi * 512 + (ci + 1) * 128, :], writes=[xk])
                    P.op("dve", lambda e, x_=x_: e.tensor_copy(out=acc[:], in_=x_[:]), reads=[xk], writes=["pacc"])
                    for s_ in range(-hw, hw):
                        if s_ == 0:
                            continue
                        if s_ < 0:
                            a = -s_
                            P.op("dve", lambda e, x_=x_, a=a: e.tensor_tensor(out=acc[:, a:S], in0=acc[:, a:S], in1=x_[:, 0:S - a], op=ALU.add), reads=[xk, "pacc"], writes=["pacc"])
                        else:
                            a = s_
                            P.op("dve", lambda e, x_=x_, a=a: e.tensor_tensor(out=acc[:, 0:S - a], in0=acc[:, 0:S - a], in1=x_[:, a:S], op=ALU.add), reads=[xk, "pacc"], writes=["pacc"])
                    P.op("dve", lambda e: e.tensor_tensor(out=acc[:], in0=acc[:], in1=pinv[:], op=ALU.mult), reads=["pacc", "pinv"], writes=["pacc"])
                    P.op("dve", lambda e, x_=x_, ci=ci: e.tensor_tensor(out=dT[:, ci, :], in0=acc[:], in1=x_[:], op=ALU.subtract), reads=["pacc", xk], writes=["pdT%d" % ci])
                for tt in range(NT):
                    pt, pk = PS(tt % 4)
                    for ci in range(4):
                        P.op("pe", lambda e, pt=pt, ci=ci, tt=tt: e.matmul(pt[:], dT[:, ci, tt * 128:(tt + 1) * 128], pw[:, ci, :], start=(ci == 0), stop=(ci == 3)),
                             reads=["pdT%d" % ci, "ppw"], writes=[pk])
                    o = po[tt % 2]; ok = "ppo%d" % (tt % 2)
                    P.op("dve", lambda e, pt=pt, o=o: e.tensor_tensor(out=o[:], in0=pt[:], in1=psc[:], op=ALU.mult), reads=[pk, "ppsc"], writes=[ok])
                    P.dma("sp", pl_s[tt * 128:(tt + 1) * 128, gi * 512:(gi + 1) * 512], o[:], reads=[ok], writes=["scr_pl"])
            P.barrier()

    gl = {}
    gl["aff_tok"] = k.sb(g, "aff_tok", [128, NT, E], F32)
    gl["slotT"] = k.sb(g, "slotT", [128, NT, E], F32)
    if on("proj"):
        stage_proj()
    if on("attn"):
        stage_attn()
    if on("hyena"):
        stage_hyena()
    for layer in range(2):
        if on("mix%d" % layer):
            if layer == 1 and on("pool"):
                stage_pool()
            stage_mix_out(layer, gl, mixer_attnhy if layer == 0 else mixer_pool)
        if on("sel%d" % layer):
            stage_select(gl)
        if on("exp%d" % layer):
            stage_experts(layer, gl)
        if on("ffn%d" % layer):
            stage_ffn_out(layer, final=(layer == 1))
    P.finish()
    k.st.close()
    return k


def _consts():
    c = {}
    L = S
    t = np.linspace(0.0, 1.0, L, dtype=np.float32)[:, None]
    bands = 16
    w_ang = (2.0 * math.pi * np.arange(L, dtype=np.float32)[:, None] / L).astype(np.float32)
    f = np.linspace(1e-4, bands - 1, bands, dtype=np.float32)[None, :]
    z = np.concatenate([t, np.cos(f * w_ang), -np.sin(f * w_ang)], axis=-1).astype(np.float32)
    c["zf"] = np.ascontiguousarray(z.T)
    c["tlin"] = np.ascontiguousarray(t.T)
    max_decay = math.log(1e-2) / 0.3
    min_decay = math.log(1e-2) / 1.5
    deltas = np.linspace(min_decay, max_decay, HW, dtype=np.float32)
    c["ndelta"] = (-np.abs(deltas)).reshape(HW, 1).astype(np.float32)
    slopes = (2.0 ** (-(8.0 / 8) * np.arange(1, 9, dtype=np.float32))).astype(np.float32)
    a = np.arange(128)[:, None]
    cq = np.arange(256)[None, :]
    rel = a + 64 - cq
    ab = np.zeros((128, 8, 3, 256), np.float32)
    for h in range(8):
        for di, (_, d) in enumerate(DIL):
            ab[:, h, di, :] = np.where(np.abs(rel) <= 64, -slopes[h] * np.abs(rel) * d, -1e30)
    c["abias"] = ab.reshape(128, 8 * 3 * 256)
    c["iota"] = np.arange(512, dtype=np.float32)[None, :]
    c["pidx"] = (np.arange(128)[:, None] + 128 * np.arange(4)[None, :]).astype(np.float32)
    c["ident"] = np.eye(128, dtype=np.float32)
    pos = np.arange(S)
    pinv = np.zeros((4, S), np.float32)
    for gi, win in enumerate((2, 4, 8, 16)):
        lo = np.clip(pos - win // 2, 0, S)
        hi = np.clip(pos + win // 2, 0, S)
        pinv[gi] = 1.0 / (hi - lo).astype(np.float32)
    c["pinv"] = pinv
    auxc = np.zeros((128, NT, 2), np.float32)
    auxc[:, :, 0] = np.arange(128)[:, None]
    auxc[:, :, 1] = np.arange(NT)[None, :]
    c["auxc"] = auxc
    c["ndrow"] = c["ndelta"].reshape(1, HW).copy()
    lag = np.arange(128)[:, None] + 128 * np.arange(NT)[None, :]
    c["tcol"] = (lag.astype(np.float64) / (S - 1)).astype(np.float32)
    m0 = np.ones((128, NT), np.float32); m0[0, 0] = 0.0
    c["m0col"] = m0
    fidx = np.arange(128)[:, None] + 128 * np.arange(33)[None, :]
    cf = np.where((fidx == 0) | (fidx == S), 1.0 / (2 * S), np.where(fidx < S, 2.0 / (2 * S), 0.0))
    c["coef"] = cf.astype(np.float32)
    n = 4224
    ii = np.arange(n, dtype=np.int64)
    prod = (ii[:, None] * ii[None, :]) % (2 * S)
    ang = prod.astype(np.float64) * (2.0 * math.pi / (2 * S))
    valid = (ii[:, None] <= S) & (ii[None, :] <= S)
    for nm, fnc in (("c", np.cos), ("s", np.sin)):
        tab = np.where(valid, fnc(ang), 0.0).astype(np.float32).astype(ml_dtypes.bfloat16)
        c["dft%sF" % nm] = np.ascontiguousarray(tab[:S, :].reshape(NT, 128, 33, 128).transpose(2, 1, 0, 3).reshape(33, 128, NT * 128))
        c["dft%sI" % nm] = np.ascontiguousarray(tab[:, :S].reshape(33, 128, 16, 256).transpose(2, 1, 0, 3).reshape(16, 128, 33 * 256))
    return c


def make_in_maps(x, mix_w_in, mix_w_out, hy_conv_w, hy_conv_b, hy_ffn_w1, hy_ffn_b1, hy_ffn_w2,
                 hy_ffn_b2, hy_ffn_w3, hy_ffn_b3, hy_ffn_w4, hy_sin_freq, hy_bias, pool_w, pool_scale,
                 ln_mix_g, ln_mix_b, ln_ffn_g, ln_ffn_b, router_w, exp_w1, exp_w3, exp_w2, ncore=NCORE):
    f = lambda a: np.ascontiguousarray(np.asarray(a, dtype=np.float32))
    shared = dict(
        w_in=f(mix_w_in[0]), w_out=f(mix_w_out[0]), convw=f(np.asarray(hy_conv_w)[0].T), convb=f(np.asarray(hy_conv_b)[0].reshape(3072, 1)),
        fw1=f(hy_ffn_w1[0]), fw2=f(hy_ffn_w2[0]), fw3=f(hy_ffn_w3[0]), fw4=f(hy_ffn_w4[0]),
        fvec=f(np.stack([np.asarray(hy_ffn_b1)[0], np.asarray(hy_ffn_b2)[0], np.asarray(hy_ffn_b3)[0], np.asarray(hy_sin_freq)[0]], axis=1)),
        hyb=f(np.asarray(hy_bias)[0].reshape(HW, 1)), poolw=f(pool_w[0]), pscale=f(np.asarray(pool_scale)[0].reshape(1, D)),
        ln=f(np.stack([np.asarray(ln_mix_g)[0], np.asarray(ln_mix_b)[0], np.asarray(ln_ffn_g)[0], np.asarray(ln_ffn_b)[0],
                       np.asarray(ln_mix_g)[1], np.asarray(ln_mix_b)[1], np.asarray(ln_ffn_g)[1], np.asarray(ln_ffn_b)[1]], axis=0)),
        rw=f(router_w), ew1=f(exp_w1), ew3=f(exp_w3), ew2=f(exp_w2),
    )
    shared.update(_consts())
    maps = []
    xa = np.asarray(x, dtype=np.float32)
    for c in range(ncore):
        b = c // 2
        m = dict(shared)
        m["x"] = np.ascontiguousarray(xa[b])
        m["xT"] = np.ascontiguousarray(xa[b].T)
        maps.append(m)
    return maps


_CACHE = {}


def kernel(**inputs):
    if "k" not in _CACHE:
        _CACHE["k"] = build()
    k = _CACHE["k"]
    maps = make_in_maps(**inputs)
    res = run_bass_kernel_spmd(k.nc, maps, core_ids=list(range(NCORE)))
    out = np.empty((4, S, D), np.float32)
    for b in range(4):
        out[b, :S // 2] = res.results[2 * b]["out"][:S // 2]
        out[b, S // 2:] = res.results[2 * b + 1]["out"][S // 2:]
    return out
```
